# Optimizing a Trainium2 kernel written in Bass

```python
import jax, jax.numpy as jnp
from jax import lax
import numpy as np

D_MODEL = 1024
BATCH = 8
SEQ = 2048
DEPTH = 1
DEC_BATCH = 128
DEC_SEQ = 1
PAST_LEN = 16384
PAGE_SIZE = 128

EXPAND = 2
D_MIX = EXPAND * D_MODEL
D_POOL = D_MIX // 2
D_RET = D_MIX - D_POOL
POOL_WINDOWS = (2, 4, 8, 16)
N_POOL_GROUPS = len(POOL_WINDOWS)
POOL_GROUP = D_POOL // N_POOL_GROUPS
POOL_BUF = max(POOL_WINDOWS) - 1
N_HEADS = 8
HEAD_DIM = D_RET // N_HEADS
D_IN_PROJ = 2 * D_POOL + 4 * D_RET
N_META = 16
CHUNK = 128
ROPE_BASE = 10000.0
EPS = 1e-6

kernel_name = "hymba_pool_retention_step"


def rms_norm(x, w):
    xf = x.astype(jnp.float32)
    y = xf * lax.rsqrt(jnp.mean(xf * xf, axis=-1, keepdims=True) + EPS)
    return (y * w.astype(jnp.float32)).astype(x.dtype)


def retention_log_decay():
    return jnp.log(1.0 - 2.0 ** (-5.0 - jnp.arange(N_HEADS, dtype=jnp.float32)))


def rotary(x, pos):
    half = HEAD_DIM // 2
    inv = ROPE_BASE ** (-jnp.arange(half, dtype=jnp.float32) / half)
    ang = pos.astype(jnp.float32)[:, None] * inv[None, :]
    cos, sin = jnp.cos(ang), jnp.sin(ang)
    x1, x2 = x[..., :half], x[..., half:]
    return jnp.concatenate([x1 * cos - x2 * sin, x1 * sin + x2 * cos], axis=-1)


def split_heads(t):
    b, L, _ = t.shape
    return t.reshape(b, L, N_HEADS, HEAD_DIM).transpose(0, 2, 1, 3)


def layer_inputs(h, norm_w, w_in, pos):
    hn = rms_norm(h, norm_w)
    proj = hn @ w_in
    o1 = D_POOL
    o2 = 2 * D_POOL
    o3 = o2 + D_RET
    o4 = o3 + D_RET
    o5 = o4 + D_RET
    u, gp, q, k, v, gr = jnp.split(proj, [o1, o2, o3, o4, o5], axis=-1)
    q = rotary(split_heads(q).astype(jnp.float32), pos)
    k = rotary(split_heads(k).astype(jnp.float32), pos) * (HEAD_DIM ** -0.5)
    v = split_heads(v).astype(jnp.float32)
    return u, gp, q, k, v, gr


def pool_mixer(u, buf, pos, w_grp, scale):
    b, L, _ = u.shape
    ext = jnp.concatenate([buf, u], axis=1).astype(jnp.float32)
    c = jnp.concatenate([jnp.zeros((b, 1, D_POOL), jnp.float32), jnp.cumsum(ext, axis=1)], axis=1)
    end = c[:, POOL_BUF + 1:]
    parts = []
    for g, w in enumerate(POOL_WINDOWS):
        sl = slice(g * POOL_GROUP, (g + 1) * POOL_GROUP)
        start = c[:, POOL_BUF + 1 - w: POOL_BUF + 1 - w + L, sl]
        cnt = jnp.minimum(pos + 1, w).astype(jnp.float32)[None, :, None]
        parts.append((end[..., sl] - start) / cnt)
    pooled = jnp.concatenate(parts, axis=-1) - u.astype(jnp.float32)
    pooled = pooled.reshape(b, L, N_POOL_GROUPS, POOL_GROUP)
    mixed = jnp.einsum("blgc,gcd->blgd", pooled, w_grp.astype(jnp.float32)).reshape(b, L, D_POOL)
    mixed = mixed * scale.astype(jnp.float32)
    return mixed.astype(u.dtype), ext[:, -POOL_BUF:].astype(u.dtype)


def retention_block(S, q, k, v, log_g):
    L = q.shape[2]
    idx = jnp.arange(L, dtype=jnp.float32)
    diff = idx[:, None] - idx[None, :]
    dmask = jnp.where(diff >= 0, jnp.exp(log_g[:, None, None] * jnp.maximum(diff, 0.0)), 0.0)
    scores = jnp.einsum("bhld,bhmd->bhlm", q, k) * dmask[None]
    inner = jnp.einsum("bhlm,bhmv->bhlv", scores, v)
    q_dec = q * jnp.exp(log_g[:, None] * (idx[None, :] + 1.0))[None, :, :, None]
    cross = jnp.einsum("bhld,bhdv->bhlv", q_dec, S)
    k_dec = k * jnp.exp(log_g[:, None] * (L - 1.0 - idx[None, :]))[None, :, :, None]
    S_new = jnp.exp(log_g * L)[None, :, None, None] * S + jnp.einsum("bhld,bhlv->bhdv", k_dec, v)
    return S_new, inner + cross


def retention_prompt(q, k, v, log_g):
    b, h, L, d = q.shape
    S0 = jnp.zeros((b, h, d, d), jnp.float32)
    S, o_meta = retention_block(S0, q[:, :, :N_META], k[:, :, :N_META], v[:, :, :N_META], log_g)
    n_chunks = (L - N_META) // CHUNK

    def to_chunks(t):
        return t[:, :, N_META:].reshape(b, h, n_chunks, CHUNK, d).transpose(2, 0, 1, 3, 4)

    def step(s, xs):
        return retention_block(s, xs[0], xs[1], xs[2], log_g)

    S, o_chunks = lax.scan(step, S, (to_chunks(q), to_chunks(k), to_chunks(v)))
    o_rest = o_chunks.transpose(1, 2, 0, 3, 4).reshape(b, h, n_chunks * CHUNK, d)
    return S, jnp.concatenate([o_meta, o_rest], axis=2)


def merge_branches(pool_o, gp, ret_o, gr, ret_norm_w, w_out):
    b, h, L, d = ret_o.shape
    rn = ret_o * lax.rsqrt(jnp.mean(ret_o * ret_o, axis=-1, keepdims=True) + EPS)
    rn = rn.transpose(0, 2, 1, 3).reshape(b, L, D_RET) * ret_norm_w.astype(jnp.float32)
    ret_y = (rn * jax.nn.silu(gr.astype(jnp.float32))).astype(pool_o.dtype)
    pool_y = pool_o * jax.nn.silu(gp)
    return jnp.concatenate([pool_y, ret_y], axis=-1) @ w_out


def setup_inputs(seed: int = 0) -> dict:
    key = jax.random.key(seed)
    ks = jax.random.split(key, 12)
    f32 = jnp.float32
    nrm = jax.random.normal
    return {
        "x_prompt": nrm(ks[0], (BATCH, SEQ, D_MODEL), f32),
        "x_sample": nrm(ks[1], (DEC_BATCH, DEC_SEQ, D_MODEL), f32),
        "state_ret": nrm(ks[2], (DEPTH, DEC_BATCH, N_HEADS, HEAD_DIM, HEAD_DIM), f32),
        "state_pool": nrm(ks[3], (DEPTH, DEC_BATCH, POOL_BUF, D_POOL), f32),
        "meta_tokens": nrm(ks[4], (N_META, D_MODEL), f32),
        "norm_w": 1.0 + 0.1 * nrm(ks[5], (DEPTH, D_MODEL), f32),
        "w_in": nrm(ks[6], (DEPTH, D_MODEL, D_IN_PROJ), f32) * D_MODEL ** -0.5,
        "w_pool": nrm(ks[7], (DEPTH, N_POOL_GROUPS, POOL_GROUP, POOL_GROUP), f32) * POOL_GROUP ** -0.5,
        "pool_scale": 1.0 + 0.1 * nrm(ks[8], (DEPTH, D_POOL), f32),
        "ret_norm_w": 1.0 + 0.1 * nrm(ks[9], (DEPTH, D_RET), f32),
        "w_out": nrm(ks[10], (DEPTH, D_MIX, D_MODEL), f32) * D_MIX ** -0.5,
        "final_norm_w": 1.0 + 0.1 * nrm(ks[11], (D_MODEL,), f32),
    }


def reference(x_prompt, x_sample, state_ret, state_pool, meta_tokens, norm_w, w_in, w_pool,
              pool_scale, ret_norm_w, w_out, final_norm_w):
    log_g = retention_log_decay()
    b_p = x_prompt.shape[0]
    meta = jnp.broadcast_to(meta_tokens.astype(x_prompt.dtype)[None], (b_p, N_META, D_MODEL))
    h_p = jnp.concatenate([meta, x_prompt], axis=1)
    pos_p = jnp.arange(h_p.shape[1], dtype=jnp.int32)
    h_s = x_sample
    pos_s = PAST_LEN + jnp.arange(x_sample.shape[1], dtype=jnp.int32)

    ret_p, ret_s, buf_p_list, buf_s_list = [], [], [], []
    for l in range(DEPTH):
        u, gp, q, k, v, gr = layer_inputs(h_p, norm_w[l], w_in[l], pos_p)
        buf0 = jnp.zeros((b_p, POOL_BUF, D_POOL), u.dtype)
        pool_o, buf_p = pool_mixer(u, buf0, pos_p, w_pool[l], pool_scale[l])
        S_p, ret_o = retention_prompt(q, k, v, log_g)
        h_p = h_p + merge_branches(pool_o, gp, ret_o, gr, ret_norm_w[l], w_out[l])
        u, gp, q, k, v, gr = layer_inputs(h_s, norm_w[l], w_in[l], pos_s)
        pool_o, buf_s = pool_mixer(u, state_pool[l].astype(u.dtype), pos_s, w_pool[l], pool_scale[l])
        S_s, ret_o = retention_block(state_ret[l].astype(jnp.float32), q, k, v, log_g)
        h_s = h_s + merge_branches(pool_o, gp, ret_o, gr, ret_norm_w[l], w_out[l])
        ret_p.append(S_p)
        ret_s.append(S_s)
        buf_p_list.append(buf_p)
        buf_s_list.append(buf_s)

    y_prompt = rms_norm(h_p, final_norm_w)[:, N_META:]
    y_sample = rms_norm(h_s, final_norm_w)
    ret_state_prompt = jnp.stack(ret_p, axis=0).astype(state_ret.dtype)
    ret_state_sample = jnp.stack(ret_s, axis=0).astype(state_ret.dtype)
    pool_buf_prompt = jnp.stack(buf_p_list, axis=0).astype(state_pool.dtype)
    pool_buf_sample = jnp.stack(buf_s_list, axis=0).astype(state_pool.dtype)
    return (y_prompt, y_sample, ret_state_prompt, ret_state_sample, pool_buf_prompt, pool_buf_sample)
```

```python
from contextlib import ExitStack
import numpy as np
import concourse.bass as bass
import concourse.mybir as mybir
from concourse.bass_utils import run_bass_kernel_spmd

F32 = mybir.dt.float32
BF16 = mybir.dt.bfloat16
ALU = mybir.AluOpType
AF = mybir.ActivationFunctionType

D = 1024
NH = 8
HD = 128
SEQ = 2048
NMETA = 16
NS = 16
PAST = 16384
EPS = 1e-6
NT = SEQ // 128
ENGS = ["pe", "act", "dve", "pool", "sp"]
WINS = (2, 4, 8, 16)
GAM = [1.0 - 2.0 ** (-5.0 - h) for h in range(NH)]


class Prog:
    def __init__(self, nc):
        self.nc = nc
        self.streams = {e: [] for e in ENGS}
        self.cnt = {}
        self.waited = {e: {} for e in ENGS}
        self.last_w = {}
        self.readers = {}
        self.stack = ExitStack()
        self.sems = {}

    def sb(self, name, shape, dt):
        return self.stack.enter_context(self.nc.sbuf_tensor(name, list(shape), dt))

    def ps(self, name, shape, dt):
        return self.stack.enter_context(self.nc.psum_tensor(name, list(shape), dt))

    def op(self, eng, fn, reads=(), writes=(), sem=None):
        semkey = sem or eng
        inc = 16 if sem is not None else 1
        deps = {}

        def add(ev):
            if ev is not None and deps.get(ev[0], 0) < ev[1]:
                deps[ev[0]] = ev[1]

        for b in reads:
            add(self.last_w.get(b))
        for b in writes:
            add(self.last_w.get(b))
            for ev in self.readers.get(b, ()):
                add(ev)
        w = self.waited[eng]
        for k, v in deps.items():
            if w.get(k, 0) < v:
                self.streams[eng].append(("wait", k, v))
                w[k] = v
        newv = self.cnt.get(semkey, 0) + inc
        self.cnt[semkey] = newv
        self.streams[eng].append(("ins", fn, semkey, inc))
        ev = (semkey, newv)
        for b in reads:
            self.readers.setdefault(b, []).append(ev)
        for b in writes:
            self.last_w[b] = ev
            self.readers[b] = []
        return ev

    def finish(self):
        for k, v in self.cnt.items():
            if self.waited["sp"].get(k, 0) < v:
                self.streams["sp"].append(("wait", k, v))
                self.waited["sp"][k] = v

    def emit(self):
        nc = self.nc
        for k in self.cnt:
            self.sems[k] = self.stack.enter_context(nc.semaphore("s_" + k))
        block = self.stack.enter_context(nc.Block())

        def run(e, name):
            for item in self.streams[name]:
                if item[0] == "wait":
                    e.wait_ge(self.sems[item[1]], item[2])
                else:
                    _, fn, semkey, inc = item
                    fn(e).then_inc(self.sems[semkey], inc)

        @block.tensor
        def _(e):
            run(e, "pe")

        @block.scalar
        def _(e):
            run(e, "act")

        @block.vector
        def _(e):
            run(e, "dve")

        @block.gpsimd
        def _(e):
            run(e, "pool")

        @block.sync
        def _(e):
            run(e, "sp")


def build_program():
    nc = bass.Bass("TRN2", target_bir_lowering=False)

    def din(name, shape):
        return nc.dram_tensor(name, list(shape), F32, kind="ExternalInput").ap()

    def dout(name, shape):
        return nc.dram_tensor(name, list(shape), F32, kind="ExternalOutput").ap()

    x = din("x", [SEQ, D])
    meta = din("meta", [NMETA, D])
    xs = din("xs", [NS, D])
    S_in = din("S_in", [NH, HD, NS, HD])
    pool_in = din("pool_in", [NS * 15, D])
    normw_col = din("normw_col", [128, 8])
    w_in = din("w_in", [D, 6 * D])
    w_pool = din("w_pool", [4, 256, 256])
    pool_scale = din("pool_scale", [1, D])
    retw_col = din("retw_col", [128, 8])
    w_out = din("w_out", [2 * D, D])
    finw = din("finw", [1, D])
    c_ident = din("c_ident", [128, 128])
    c_cs = din("c_cs", [NT + 2, 128, 192])
    c_dec = din("c_dec", [128, 48])
    c_mask = din("c_mask", [128, 256])
    c_band = din("c_band", [128, 3 * 4 * 128])
    c_csel = din("c_csel", [128, 2 * 4 * 16])
    c_cu = din("c_cu", [128, 4 * 16])
    c_oh = din("c_oh", [128, 256])

    y = dout("y", [SEQ, D])
    ys = dout("ys", [NS, D])
    Sp = dout("Sp", [NH, HD, HD])
    Ss = dout("Ss", [NH, HD, NS, HD])
    pbp = dout("pbp", [15, D])
    pbs = dout("pbs", [NS, 15, D])

    P = Prog(nc)
    w_in_bf = P.sb("w_in_bf", [128, 8, 6 * D], BF16)
    w_out_bf = P.sb("w_out_bf", [128, 16, D], BF16)
    w_pool_bf = P.sb("w_pool_bf", [128, 2048], BF16)
    ident = P.sb("ident", [128, 128], BF16)
    normw = P.sb("normw", [128, 8], F32)
    retwc = P.sb("retwc", [128, 8], F32)
    finw_b = P.sb("finw_b", [128, D], F32)
    dec = P.sb("dec", [128, 48], F32)
    mask = P.sb("mask", [128, 256], F32)
    band = P.sb("band", [128, 3 * 4 * 128], BF16)
    csel = P.sb("csel", [128, 128], BF16)
    cu = P.sb("cu", [128, 64], BF16)
    oh = P.sb("oh", [128, 256], BF16)
    nh = P.sb("nh", [128, 8], F32)
    cs = [P.sb("cs%d" % i, [128, 192], F32) for i in range(2)]
    xall = P.sb("xall", [128, 3 * D], F32)
    xb = [xall[:, i * D:(i + 1) * D] for i in range(3)]
    Fb = P.sb("Fb", [128, 2 * D], F32)
    F = [Fb[:, i * D:(i + 1) * D] for i in range(2)]
    hn = P.sb("hn", [128, D], BF16)
    W0 = P.sb("W0", [128, D], BF16)
    hnTall = P.sb("hnTall", [128, 2 * D], BF16)
    hnTs = [hnTall[:, i * D:(i + 1) * D] for i in range(2)]
    uall = P.sb("uall", [128, 3 * D], BF16)
    u_bf = [uall[:, i * D:(i + 1) * D] for i in range(3)]
    qd = P.sb("qd", [128, D], BF16)
    kd = P.sb("kd", [128, D], BF16)
    qkT = P.sb("qkT", [128, 2 * D], BF16)
    qT = qkT[:, 0:D]
    kT = qkT[:, D:2 * D]
    v_bf = P.sb("v_bf", [128, D], BF16)
    sgpT = P.sb("sgpT", [128, D], BF16)
    G = P.sb("G", [128, D], BF16)
    ret_y = P.sb("ret_y", [128, D], BF16)
    yT = P.sb("yT", [128, 2 * D], BF16)
    S_f32 = P.sb("S_f32", [128, D], F32)
    S_bf = P.sb("S_bf", [128, D], BF16)
    ss_x = P.sb("ss_x", [128, 1], F32)
    rs_x = P.sb("rs_x", [128, 1], F32)
    ss_r = P.sb("ss_r", [128, 8], F32)
    rs_r = P.sb("rs_r", [128, 8], F32)
    ss_f = P.sb("ss_f", [128, 1], F32)
    rs_f = P.sb("rs_f", [128, 1], F32)
    pp = [P.ps("pp%d" % i, [128, D], F32) for i in range(4)]
    ppb = [t[:].bitcast(BF16) for t in pp]
    pj = [0]

    forced = []

    def next_pair():
        if forced:
            i = forced.pop(0)
        else:
            i = pj[0] % 4
            pj[0] += 1
        return pp[i], ppb[i], "pp%d" % i

    uniq = [0]

    def dma(eng, out, in_, reads, writes, sem=None):
        if sem is None:
            uniq[0] += 1
            sem = "d_u%d" % uniq[0]
        P.op(eng, lambda e: e.dma_start(out=out, in_=in_), reads=reads, writes=writes, sem=sem)

    dma("pool", ident[:], c_ident[:, :], [], ["ident"])
    dma("sp", normw[:], normw_col[:, :], [], ["normw"])
    dma("sp", retwc[:], retw_col[:, :], [], ["retwc"])
    dma("sp", dec[:], c_dec[:, :], [], ["dec"])
    dma("sp", mask[:], c_mask[:, :], [], ["mask"])
    P.op("dve", lambda e: e.memset(nh[:], -0.5), writes=["nh"])
    w_in_v = w_in.rearrange("(kc p) n -> p kc n", p=128)
    for blk in (0, 1, 6, 7, 8, 9, 4, 5, 10, 11, 2, 3):
        dma("pool", w_in_bf[:, :, blk * 512:(blk + 1) * 512], w_in_v[:, :, blk * 512:(blk + 1) * 512],
            [], ["win%d" % blk])
        if blk == 1:
            dma("pool", band[:], c_band[:, :], [], ["band"])
            dma("pool", csel[:], c_csel[:, :], [], ["csel"])
            dma("pool", cu[:], c_cu[:, :], [], ["cu"])
            dma("pool", oh[:], c_oh[:, :], [], ["oh"])
    dma("sp", finw_b[:], finw[0:1, :].broadcast_to([128, D]), [], ["finw_b"])
    dma("sp", Fb[:, 0:2048].rearrange("p (g cc d) -> p g cc d", g=4, cc=2),
        w_pool.rearrange("g (cc p) d -> p g cc d", p=128), [], ["F0", "F1"])
    dma("sp", xb[2], pool_scale[0:1, :].broadcast_to([128, D]), [], ["xb2"])
    P.op("dve", lambda e: e.tensor_tensor(
        out=w_pool_bf[:].rearrange("p (g cc d) -> p g cc d", g=4, cc=2),
        in0=Fb[:, 0:2048].rearrange("p (g cc d) -> p g cc d", g=4, cc=2),
        in1=xb[2].rearrange("p (g d) -> p g d", g=4).unsqueeze(2).broadcast_to([128, 4, 2, 256]),
        op=ALU.mult), reads=["F0", "F1", "xb2"], writes=["w_pool_bf"])
    w_out_v = w_out.rearrange("(kc p) n -> p kc n", p=128)

    def load_w_out():
        for blk in range(4):
            dma("pool", w_out_bf[:, blk * 4:(blk + 1) * 4, :], w_out_v[:, blk * 4:(blk + 1) * 4, :],
                [], ["wout%d" % blk])

    def rstd(ss, rs, n, T, inv_n, key_ss, key_rs):
        P.op("dve", lambda e: e.tensor_scalar(out=rs[:T, 0:n], in0=ss[:T, 0:n], scalar1=inv_n, scalar2=EPS,
                                              op0=ALU.mult, op1=ALU.add), reads=[key_ss], writes=[key_rs])
        P.op("pool", lambda e: e.tensor_tensor(out=rs[:T, 0:n], in0=rs[:T, 0:n], in1=nh[:T, 0:n], op=ALU.pow),
             reads=[key_rs, "nh"], writes=[key_rs])

    def act_evac(T, pt, pk, dst, dkey, func=AF.Copy, ncol=D):
        P.op("act", lambda e: e.activation(out=dst[:T, 0:ncol], in_=pt[:T, 0:ncol], func=func),
             reads=[pk], writes=[dkey])

    def dve_copy(rows, ncol, src, skeys, dst, dkeys):
        P.op("dve", lambda e: e.tensor_copy(out=dst[:rows, 0:ncol], in_=src[:rows, 0:ncol]),
             reads=list(skeys), writes=list(dkeys))

    def load_x(T, xsrc, cs_idx, xi, ci):
        dma("sp", xb[xi][:T, :], xsrc, [], ["xb%d" % xi], "d_x%d" % xi)
        dma("sp", cs[ci][:T, :], c_cs[cs_idx, 0:T, :], [], ["cs%d" % ci], "d_cs%d" % ci)

    def stage_A1(T, xi):
        xk = "xb%d" % xi
        P.op("act", lambda e: e.activation(out=hn[:T, :], in_=xb[xi][:T, :], func=AF.Square, accum_out=ss_x[:T, :]),
             reads=[xk], writes=["hn", "ss_x"])
        rstd(ss_x, rs_x, 1, T, 1.0 / D, "ss_x", "rs_x")
        P.op("dve", lambda e: e.tensor_scalar(out=hn[:T, :], in0=xb[xi][:T, :], scalar1=rs_x[:T, 0:1], scalar2=None,
                                              op0=ALU.mult), reads=[xk, "rs_x"], writes=["hn"])

    def stage_A2(T, hi):
        hnT = hnTs[hi]
        pt, ptb, pk = next_pair()

        def tr(e):
            r = None
            for kc in range(8):
                r = e.transpose(out=ptb[:, kc * T:(kc + 1) * T], in_=hn[:T, kc * 128:(kc + 1) * 128],
                                identity=ident[:T, :T])
            return r
        P.op("pe", tr, reads=["hn", "ident"], writes=[pk])
        yield

        def ev(e):
            r = None
            for kc in range(8):
                r = e.tensor_scalar(out=hnT[:, kc * T:(kc + 1) * T], in0=ptb[:, kc * T:(kc + 1) * T],
                                    scalar1=normw[:, kc:kc + 1], scalar2=None, op0=ALU.mult)
            return r
        P.op("dve", ev, reads=[pk, "normw"], writes=["hnT%d" % hi])
        yield

    def proj_A(T, col0, hi):
        hnT = hnTs[hi]
        pt, ptb, pk = next_pair()

        def mm(e):
            r = None
            for nt in range(2):
                for kc in range(8):
                    c0 = col0 + nt * 512
                    r = e.matmul(pt[:T, nt * 512:(nt + 1) * 512], lhsT=hnT[:, kc * T:(kc + 1) * T],
                                 rhs=w_in_bf[:, kc, c0:c0 + 512], start=(kc == 0), stop=(kc == 7))
            return r
        b0 = col0 // 512
        P.op("pe", mm, reads=["hnT%d" % hi, "win%d" % b0, "win%d" % (b0 + 1)], writes=[pk])
        return pt, pk

    def rotary(T, pt, pk, ci, dslot, dst, dkey, fa=0):
        Fa, fak = F[fa], "F%d" % fa

        def evac(e):
            r = None
            for h in range(NH):
                r = e.activation(out=Fa[:T, h * 128:(h + 1) * 128], in_=pt[:T, h * 128:(h + 1) * 128],
                                 func=AF.Copy, scale=dec[:T, dslot * 8 + h:dslot * 8 + h + 1])
            return r
        P.op("act", evac, reads=[pk, "dec"], writes=[fak])
        yield
        f0 = Fa[:T, :].rearrange("p (h two j) -> p h two j", h=8, two=2)
        f1 = F[1][:T, :].rearrange("p (h two j) -> p h two j", h=8, two=2)
        cosb = cs[ci][:T, 0:64].unsqueeze(1).unsqueeze(1).broadcast_to([T, 8, 2, 64])
        sinb = cs[ci][:T, 64:128].unsqueeze(1).broadcast_to([T, 8, 64])
        nsinb = cs[ci][:T, 128:192].unsqueeze(1).broadcast_to([T, 8, 64])

        def rot(e):
            e.tensor_tensor(out=f1[:, :, 0, :], in0=f0[:, :, 1, :], in1=nsinb, op=ALU.mult)
            return e.tensor_tensor(out=f1[:, :, 1, :], in0=f0[:, :, 0, :], in1=sinb, op=ALU.mult)
        P.op("dve", rot, reads=[fak, "cs%d" % ci], writes=["F1"])
        P.op("dve", lambda e: e.tensor_tensor(out=f0, in0=f0, in1=cosb, op=ALU.mult),
             reads=[fak, "F1", "cs%d" % ci], writes=[fak])
        yield
        P.op("dve", lambda e: e.tensor_tensor(out=dst[:T, :], in0=Fa[:T, :], in1=F[1][:T, :], op=ALU.add),
             reads=[fak, "F1"], writes=[dkey])
        yield

    def transposes(T, src, skey, dst, dkey, col0=0, scale_col=None, sckey=None):
        pt, ptb, pk = next_pair()

        def tr(e):
            r = None
            for h in range(NH):
                r = e.transpose(out=ptb[:, h * T:(h + 1) * T], in_=src[:T, h * 128:(h + 1) * 128],
                                identity=ident[:T, :T])
            return r
        P.op("pe", tr, reads=(list(skey) if isinstance(skey, (list, tuple)) else [skey]) + ["ident"], writes=[pk])
        if scale_col is None:
            P.op("act", lambda e: e.activation(out=dst[:, col0:col0 + 8 * T], in_=ptb[:, 0:8 * T], func=AF.Copy),
                 reads=[pk], writes=[dkey])
        else:
            def ev(e):
                r = None
                for h in range(NH):
                    r = e.tensor_scalar(out=dst[:, col0 + h * 128:col0 + h * 128 + T], in0=ptb[:, h * T:(h + 1) * T],
                                        scalar1=scale_col[:, h:h + 1], scalar2=None, op0=ALU.mult)
                return r
            P.op("dve", ev, reads=[pk, sckey], writes=[dkey])

    def out_proj_and_store(T, xi, ydst):
        xk = "xb%d" % xi
        pt, ptb, pk = next_pair()

        def mm(e):
            r = None
            for nt in range(2):
                for kc in range(16):
                    r = e.matmul(pt[:T, nt * 512:(nt + 1) * 512], lhsT=yT[:, kc * 128:kc * 128 + T],
                                 rhs=w_out_bf[:, kc, nt * 512:(nt + 1) * 512], start=(kc == 0), stop=(kc == 15))
            return r
        P.op("pe", mm, reads=["yT", "wout0", "wout1", "wout2", "wout3"], writes=[pk])
        out_post(T, xi, pt, pk, ydst)

    def out_post(T, xi, pt, pk, ydst):
        xk = "xb%d" % xi
        P.op("dve", lambda e: e.tensor_tensor(out=xb[xi][:T, :], in0=xb[xi][:T, :], in1=pt[:T, :], op=ALU.add),
             reads=[xk, pk], writes=[xk])
        P.op("act", lambda e: e.activation(out=ret_y[:T, :], in_=xb[xi][:T, :], func=AF.Square, accum_out=ss_f[:T, :]),
             reads=[xk], writes=["ret_yA", "ret_yB", "ss_f"])
        rstd(ss_f, rs_f, 1, T, 1.0 / D, "ss_f", "rs_f")
        P.op("dve", lambda e: e.scalar_tensor_tensor(out=xb[xi][:T, :], in0=xb[xi][:T, :], scalar=rs_f[:T, 0:1],
                                                     in1=finw_b[:T, :], op0=ALU.mult, op1=ALU.mult),
             reads=[xk, "rs_f", "finw_b"], writes=[xk])
        dma("sp", ydst, xb[xi][:T, :], [xk], [], "d_y%d" % xi)

    def ret_norm_and_gate(T, pr, prk):
        halves = [("A", range(0, 4)), ("B", range(4, 8))]
        for tag, hr in halves:
            def sq(e, hr=hr):
                r = None
                for h in hr:
                    r = e.activation(out=ret_y[:T, h * 128:(h + 1) * 128], in_=pr[:T, h * 128:(h + 1) * 128],
                                     func=AF.Square, accum_out=ss_r[:T, h:h + 1])
                return r
            P.op("act", sq, reads=[prk], writes=["ret_y" + tag, "ss_r" + tag])
        yield
        for tag, hr in halves:
            c0 = hr[0]
            P.op("dve", lambda e, c0=c0: e.tensor_scalar(out=rs_r[:T, c0:c0 + 4], in0=ss_r[:T, c0:c0 + 4],
                                                        scalar1=1.0 / HD, scalar2=EPS, op0=ALU.mult, op1=ALU.add),
                 reads=["ss_r" + tag], writes=["rs_r" + tag])
            P.op("pool", lambda e, c0=c0: e.tensor_tensor(out=rs_r[:T, c0:c0 + 4], in0=rs_r[:T, c0:c0 + 4],
                                                         in1=nh[:T, 0:4], op=ALU.pow),
                 reads=["rs_r" + tag, "nh"], writes=["rs_r" + tag])
        yield
        for tag, hr in halves:
            def gate(e, hr=hr):
                r = None
                for h in hr:
                    r = e.scalar_tensor_tensor(out=ret_y[:T, h * 128:(h + 1) * 128], in0=pr[:T, h * 128:(h + 1) * 128],
                                               scalar=rs_r[:T, h:h + 1], in1=G[:T, h * 128:(h + 1) * 128],
                                               op0=ALU.mult, op1=ALU.mult)
                return r
            P.op("dve", gate, reads=[prk, "rs_r" + tag, "G"], writes=["ret_y" + tag])
        yield

    def ret_yT(T):
        transposes(T, ret_y, ["ret_yA", "ret_yB"], yT, "yT", col0=8 * 128, scale_col=retwc, sckey="retwc")

    def proj_gr(T, hi):
        pt, pk = proj_A(T, 5 * D, hi)
        act_evac(T, pt, pk, G, "G", func=AF.Silu)

    def proj_gp(T, hi):
        hnT = hnTs[hi]
        pt2, ptb2, pk2 = next_pair()

        def mmB(e):
            r = None
            for fc in range(8):
                for kc in range(8):
                    r = e.matmul(pt2[:, fc * T:(fc + 1) * T], lhsT=w_in_bf[:, kc, D + fc * 128:D + (fc + 1) * 128],
                                 rhs=hnT[:, kc * T:(kc + 1) * T], start=(kc == 0), stop=(kc == 7))
            return r
        P.op("pe", mmB, reads=["hnT%d" % hi, "win2", "win3"], writes=[pk2])
        P.op("act", lambda e: e.activation(out=sgpT[:, 0:8 * T], in_=pt2[:, 0:8 * T], func=AF.Silu),
             reads=[pk2], writes=["sgpT"])

    def pooled_evac(T, ppool, pkpool):
        P.op("dve", lambda e: e.tensor_copy(out=W0[:, 0:8 * T], in_=ppool[:, 0:8 * T]), reads=[pkpool], writes=["W0"])

    def mix_and_gate(T, src=None, skey="W0"):
        src = W0 if src is None else src
        pt, ptb, pk = next_pair()

        def mm(e):
            r = None
            for i2 in range(8):
                g, dd = i2 // 2, i2 % 2
                for cc in range(2):
                    o = (g * 2 + cc) * 256 + dd * 128
                    r = e.matmul(pt[:, i2 * T:(i2 + 1) * T], lhsT=w_pool_bf[:, o:o + 128],
                                 rhs=src[:, (2 * g + cc) * T:(2 * g + cc + 1) * T], start=(cc == 0), stop=(cc == 1))
            return r
        P.op("pe", mm, reads=[skey, "w_pool_bf"], writes=[pk])
        P.op("dve", lambda e: e.tensor_tensor(
            out=yT[:, 0:8 * 128].rearrange("p (h t) -> p h t", h=8)[:, :, 0:T],
            in0=pt[:, 0:8 * T].rearrange("p (h t) -> p h t", h=8),
            in1=sgpT[:, 0:8 * T].rearrange("p (h t) -> p h t", h=8), op=ALU.mult),
            reads=[pk, "sgpT"], writes=["yT"])

    def mm_scores(T):
        pt, ptb, pk = next_pair()

        def mm(e):
            r = None
            for h in range(NH):
                r = e.matmul(pt[:T, h * T:(h + 1) * T], lhsT=kT[:, h * T:(h + 1) * T], rhs=qT[:, h * T:(h + 1) * T],
                             start=True, stop=True)
            return r
        P.op("pe", mm, reads=["kT", "qT"], writes=[pk])
        return pt, pk

    def ev_scores(T, pt, pk, scal, mcol0):
        def ev(e):
            r = None
            for h in range(NH):
                r = e.scalar_tensor_tensor(out=W0[:T, h * T:(h + 1) * T], in0=pt[:T, h * T:(h + 1) * T],
                                           scalar=float(scal[h]), in1=mask[:T, mcol0:mcol0 + T],
                                           op0=ALU.mult, op1=ALU.mult)
            return r
        P.op("dve", ev, reads=[pk, "mask"], writes=["W0"])

    def mm_dS(T):
        pt, ptb, pk = next_pair()

        def mm(e):
            r = None
            for h in range(NH):
                r = e.matmul(pt[:, h * 128:(h + 1) * 128], lhsT=kd[:T, h * 128:(h + 1) * 128],
                             rhs=v_bf[:T, h * 128:(h + 1) * 128], start=True, stop=True)
            return r
        P.op("pe", mm, reads=["kd", "v_bf"], writes=[pk])
        return pt, pk

    bandv = band[:].rearrange("p (s g t) -> p s g t", s=3, g=4)

    def tile_T(idx):
        return 128 if 1 <= idx <= NT else 16

    def tile_load(idx):
        T = tile_T(idx)
        if idx == 0:
            src = meta[:, :]
        elif idx <= NT:
            src = x[(idx - 1) * 128:idx * 128, :]
        else:
            src = xs[:, :]
        load_x(T, src, idx, idx % 3, idx % 2)

    def P1_gens(idx):
        T = tile_T(idx)
        ci, ui, hi = idx % 2, idx % 3, idx % 2
        ukey = "u_bf%d" % ui

        def job_u():
            pt, pk = proj_A(T, 0, hi)
            yield
            if idx >= NT:
                act_evac(T, pt, pk, F[1], "F1")
                dve_copy(T, D, F[1], ["F1"], u_bf[ui], [ukey])
                if idx == NT:
                    dma("sp", pbp[:, :], F[1][113:128, :], ["F1"], [])
                else:
                    dma("sp", pbs[:, 14, :], F[1][:T, :], ["F1"], [])
                    dma("sp", pbs[:, 0:14, :], pool_in.rearrange("(s j) d -> s j d", j=15)[:, 1:15, :], [], [])
            else:
                act_evac(T, pt, pk, u_bf[ui], ukey)
            yield

        def job_q():
            pt, pk = proj_A(T, 2 * D, hi)
            yield
            yield from rotary(T, pt, pk, ci, 0 if idx <= NT else 4, qd, "qd")

        def job_k():
            pt, pk = proj_A(T, 3 * D, hi)
            yield
            yield from rotary(T, pt, pk, ci, 3 if idx == 0 else (1 if idx <= NT else 5), kd, "kd")

        def job_v():
            pt, pk = proj_A(T, 4 * D, hi)
            yield
            act_evac(T, pt, pk, v_bf, "v_bf")
            yield

        def job_gr():
            proj_gr(T, hi)
            yield
            yield

        def job_gp():
            proj_gp(T, hi)
            yield
            yield
        d = {"u": job_u(), "k": job_k(), "v": job_v()}
        if idx != 0:
            d.update({"q": job_q(), "gr": job_gr(), "gp": job_gp()})
        return d

    def P2_gens(c):
        T = 128
        xi = c % 3
        ucur, ukey = u_bf[c % 3], "u_bf%d" % (c % 3)
        uprev, upkey = u_bf[(c - 1) % 3], "u_bf%d" % ((c - 1) % 3)

        def j_tr():
            transposes(T, qd, "qd", qT, "qT")
            transposes(T, kd, "kd", kT, "kT")
            yield

        def j_ds_scores():
            pd, pdk = mm_dS(T)
            yield
            pt, pk = mm_scores(T)
            ev_scores(T, pt, pk, [g ** (-128.0) for g in GAM], 0)

            def upd(e):
                r = None
                for h in range(NH):
                    r = e.scalar_tensor_tensor(out=S_f32[:, h * 128:(h + 1) * 128], in0=S_f32[:, h * 128:(h + 1) * 128],
                                               scalar=float(GAM[h] ** 128.0), in1=pd[:, h * 128:(h + 1) * 128],
                                               op0=ALU.mult, op1=ALU.add)
                return r
            P.op("dve", upd, reads=[pdk, "S_f32"], writes=["S_f32"])
            yield

        def j_ret():
            pr, prb, prk = next_pair()

            def mm_ret(e):
                r = None
                for h in range(NH):
                    e.matmul(pr[:T, h * 128:(h + 1) * 128], lhsT=W0[:T, h * T:(h + 1) * T],
                             rhs=v_bf[:T, h * 128:(h + 1) * 128], start=True, stop=False)
                    r = e.matmul(pr[:T, h * 128:(h + 1) * 128], lhsT=qT[:, h * T:(h + 1) * T],
                                 rhs=S_bf[:, h * 128:(h + 1) * 128], start=False, stop=True)
                return r
            P.op("pe", mm_ret, reads=["W0", "v_bf", "qT", "S_bf"], writes=[prk])
            yield
            g = ret_norm_and_gate(T, pr, prk)
            next(g)
            yield
            next(g)
            yield
            next(g)
            yield
            dve_copy(128, D, S_f32, ["S_f32"], S_bf, ["S_bf"])
            if c == NT:
                dma("sp", Sp.rearrange("h d v -> d h v"), S_f32[:].rearrange("p (h v) -> p h v", h=8), ["S_f32"], [])
            yield

        def j_pool():
            pq, pqb, pqk = next_pair()
            Kp, slot = (NMETA, 2) if c == 1 else (128, 1)

            def mm_pool(e):
                r = None
                for j in range(8):
                    g = j // 2
                    e.matmul(pq[:, j * T:(j + 1) * T], lhsT=uprev[:Kp, j * 128:(j + 1) * 128],
                             rhs=bandv[:Kp, slot, g, :], start=True, stop=False)
                    r = e.matmul(pq[:, j * T:(j + 1) * T], lhsT=ucur[:T, j * 128:(j + 1) * 128],
                                 rhs=bandv[:T, 0, g, :], start=False, stop=True)
                return r
            P.op("pe", mm_pool, reads=[ukey, upkey, "band"], writes=[pqk])
            pooled_evac(T, pq, pqk)
            yield

        def j_mix():
            mix_and_gate(T)
            yield
            ret_yT(T)
            yield

        def j_out():
            xk = "xb%d" % xi
            pt, ptb, pk = next_pair()

            def mm(e):
                r = None
                for nt in range(2):
                    for kc in range(16):
                        r = e.matmul(pt[:T, nt * 512:(nt + 1) * 512], lhsT=yT[:, kc * 128:kc * 128 + T],
                                     rhs=w_out_bf[:, kc, nt * 512:(nt + 1) * 512], start=(kc == 0), stop=(kc == 15))
                return r
            P.op("pe", mm, reads=["yT", "wout0", "wout1", "wout2", "wout3"], writes=[pk])
            yield
            out_post(T, xi, pt, pk, y[(c - 1) * 128:c * 128, :])
            yield
        return {"tr": j_tr(), "ds": j_ds_scores(), "ret": j_ret(), "pool": j_pool(), "mix": j_mix(), "out": j_out()}

    def P2_meta():
        pt, pk = mm_dS(NMETA)
        dve_copy(128, D, pt, [pk], S_f32, ["S_f32"])
        dve_copy(128, D, S_f32, ["S_f32"], S_bf, ["S_bf"])

    SinB = [F[0], F[1], S_f32[:, :]]
    sinK = ["F0", "F1", "S_f32"]

    def sample_load(j):
        h, half, b3 = j // 2, j % 2, j % 3
        dma("sp", SinB[b3].rearrange("p (s v) -> p s v", s=8),
            S_in[h, :, 8 * half:8 * half + 8, :], [], [sinK[b3]], "d_sl%d" % b3)

    def sample_pool_state_load():
        spb = S_f32[:].bitcast(BF16)
        pin = pool_in.rearrange("(a r) d -> r a d", a=2)
        dma("pool", spb[:120, 0:2048].rearrange("p (a d) -> p a d", a=2), pin[:, :, :], [], ["S_f32"])

    def P2_sample():
        T = NS
        idx = NT + 1
        xi = idx % 3
        transposes(T, qd, "qd", qT, "qT")
        transposes(T, kd, "kd", kT, "kT")
        pt, pk = mm_scores(T)
        ev_scores(T, pt, pk, [1.0 / g for g in GAM], 128)
        Qm = uall[:, 0:2048]
        Sin, sink = SinB, sinK
        Sout, soutk = [xb[0], xb[1]], ["xb0", "xb1"]
        Sbf, sbk = [hnTs[0], hnTs[1]], ["hnT0", "hnT1"]
        Rh, rhk = [qT, kT], ["qT", "kT"]
        P.op("dve", lambda e: e.tensor_tensor(
            out=Qm.rearrange("p (h s t) -> p h s t", h=8, s=16),
            in0=qT[:, 0:128].rearrange("p (h t) -> p h t", h=8).unsqueeze(2).broadcast_to([128, 8, 16, 16]),
            in1=oh[:, :].rearrange("p (s t) -> p s t", s=16).unsqueeze(1).broadcast_to([128, 8, 16, 16]),
            op=ALU.mult), reads=["qT", "oh"], writes=["u_bf0", "u_bf1"])
        spb = S_f32[:].bitcast(BF16)
        cselv = csel[:].rearrange("p (a g s) -> p a g s", a=2, g=4)
        cuv = cu[:].rearrange("p (g s) -> p g s", g=4)
        us = u_bf[idx % 3]
        ridx = pj[0] % 4
        pr, prb, prk = next_pair()
        pbr = (ridx + 3) % 4

        def pool_branch():
            forced.append(pbr)
            pq, pqb, pqk = next_pair()

            def mm_pool_s(e):
                r = None
                for j in range(8):
                    g = j // 2
                    e.matmul(pq[:, j * T:(j + 1) * T], lhsT=spb[:120, j * 128:(j + 1) * 128],
                             rhs=cselv[:120, 0, g, :], start=True, stop=False)
                    e.matmul(pq[:, j * T:(j + 1) * T], lhsT=spb[:120, 1024 + j * 128:1024 + (j + 1) * 128],
                             rhs=cselv[:120, 1, g, :], start=False, stop=False)
                    r = e.matmul(pq[:, j * T:(j + 1) * T], lhsT=us[:T, j * 128:(j + 1) * 128], rhs=cuv[:T, g, :],
                                 start=False, stop=True)
                return r
            P.op("pe", mm_pool_s, reads=["S_f32", "csel", "cu", "u_bf%d" % (idx % 3)], writes=[pqk])
            P.op("dve", lambda e: e.tensor_copy(out=hn[:, 0:8 * T], in_=pq[:, 0:8 * T]), reads=[pqk], writes=["hn"])
            forced.append(pbr)
            mix_and_gate(T, src=hn, skey="hn")
        qmk2 = ["u_bf0", "u_bf1"]
        NJ = 2 * NH

        def cast(j):
            bb, b3 = j % 2, j % 3
            P.op("act", lambda e: e.activation(out=Sbf[bb], in_=Sin[b3], func=AF.Copy),
                 reads=[sink[b3]], writes=[sbk[bb]])

        def ret_s(j):
            h, half, bb = j // 2, j % 2, j % 2

            def mm_ret_s(e):
                if half == 0:
                    e.matmul(pr[:T, h * 128:(h + 1) * 128], lhsT=W0[:T, h * T:(h + 1) * T],
                             rhs=v_bf[:T, h * 128:(h + 1) * 128], start=True, stop=False)
                r = None
                for sl in range(8):
                    s_ = 8 * half + sl
                    r = e.matmul(pr[:T, h * 128:(h + 1) * 128], lhsT=Qm[:, (h * 16 + s_) * 16:(h * 16 + s_ + 1) * 16],
                                 rhs=Sbf[bb][:, sl * 128:(sl + 1) * 128], start=False, stop=(s_ == NS - 1))
                return r
            P.op("pe", mm_ret_s, reads=["W0", "v_bf", sbk[bb]] + qmk2, writes=[prk])

        def rh_build(j):
            h, half, bb = j // 2, j % 2, j % 2
            P.op("dve", lambda e: e.tensor_tensor(
                out=Rh[bb][:T, :].rearrange("p (s v) -> p s v", s=8),
                in0=v_bf[:T, h * 128:(h + 1) * 128].unsqueeze(1).broadcast_to([T, 8, 128]),
                in1=mask[:T, 128 + 8 * half:128 + 8 * half + 8].unsqueeze(2).broadcast_to([T, 8, 128]),
                op=ALU.mult), reads=["v_bf", "mask"], writes=[rhk[bb]])

        def ds_upd(j):
            h, half, bb = j // 2, j % 2, j % 2
            pi = (ridx + 1 + j % 2) % 4
            pa, pak = pp[pi], "pp%d" % pi

            def mm_ds_s(e):
                r = None
                for q2 in range(2):
                    r = e.matmul(pa[:, q2 * 512:(q2 + 1) * 512], lhsT=kd[:T, h * 128:(h + 1) * 128],
                                 rhs=Rh[bb][:T, q2 * 512:(q2 + 1) * 512], start=True, stop=True)
                return r
            P.op("pe", mm_ds_s, reads=["kd", rhk[bb]], writes=[pak])
            b3 = j % 3
            P.op("dve", lambda e: e.scalar_tensor_tensor(
                out=Sout[bb], in0=Sin[b3], scalar=float(GAM[h]), in1=pa[:, :], op0=ALU.mult, op1=ALU.add),
                reads=[pak, sink[b3]], writes=[soutk[bb]])
            dma("pool", Ss[h, :, 8 * half:8 * half + 8, :],
                Sout[bb].rearrange("p (s v) -> p s v", s=8), [soutk[bb]], [], "d_ss%d" % bb)

        pool_branch()
        po, pok = pp[pbr], "pp%d" % pbr
        sample_load(2)
        cast(0)
        ret_s(0)
        rh_build(0)
        for j in range(NJ):
            if j + 1 < NJ:
                rh_build(j + 1)
            ds_upd(j)
            if j + 3 < NJ:
                sample_load(j + 3)
            if j + 1 < NJ:
                cast(j + 1)
                ret_s(j + 1)
            if j == 7:
                def mm_o1(e):
                    r = None
                    for nt in range(2):
                        for kc in range(8):
                            r = e.matmul(po[:T, nt * 512:(nt + 1) * 512], lhsT=yT[:, kc * 128:kc * 128 + T],
                                         rhs=w_out_bf[:, kc, nt * 512:(nt + 1) * 512], start=(kc == 0), stop=False)
                    return r
                P.op("pe", mm_o1, reads=["yT", "wout0", "wout1"], writes=[pok])
        pj[0] = ridx + 1
        for _ in ret_norm_and_gate(T, pr, prk):
            pass
        ret_yT(T)

        def mm_o2(e):
            r = None
            for nt in range(2):
                for kc in range(8, 16):
                    r = e.matmul(po[:T, nt * 512:(nt + 1) * 512], lhsT=yT[:, kc * 128:kc * 128 + T],
                                 rhs=w_out_bf[:, kc, nt * 512:(nt + 1) * 512], start=False, stop=(kc == 15))
            return r
        P.op("pe", mm_o2, reads=["yT", "wout2", "wout3"], writes=[pok])
        out_post(T, xi, po, pok, ys[:, :])

    def run(g):
        next(g, None)

    def full(g):
        for _ in g:
            pass

    tile_load(0)
    tile_load(1)
    stage_A1(tile_T(0), 0)
    full(stage_A2(tile_T(0), 0))
    g0 = P1_gens(0)
    for k in ("u", "k", "v"):
        full(g0[k])
    stage_A1(tile_T(1), 1)
    full(stage_A2(tile_T(1), 1))
    load_w_out()
    P2_meta()
    tile_load(2)
    g1 = P1_gens(1)
    stage_A1(tile_T(2), 2 % 3)
    for k in ("u", "q", "k", "v", "gr", "gp"):
        full(g1[k])
    full(stage_A2(tile_T(2), 2 % 2))
    for idx in range(1, NT + 1):
        nA = idx + 2 if idx + 2 <= NT + 1 else None
        if nA is not None:
            tile_load(nA)
        p1 = P1_gens(idx + 1)
        p2 = P2_gens(idx)
        a2 = stage_A2(tile_T(nA), nA % 2) if nA is not None else iter(())
        pj[0] = 0
        forced.extend([0, 1]); run(p2["tr"])
        forced.append(2); run(p2["ds"])
        forced.append(3); run(p1["u"])
        forced.append(0); run(p2["ds"])
        if nA is not None:
            stage_A1(tile_T(nA), nA % 3)
        run(p1["u"])
        forced.append(1); run(p1["q"])
        forced.append(2); run(p2["ret"])
        forced.append(0); run(p2["pool"])
        run(p2["ret"])
        run(p1["q"])
        run(p2["ret"])
        run(p2["ret"])
        run(p1["q"])
        forced.append(3); run(p1["k"])
        forced.append(0); run(p2["mix"])
        run(p1["q"])
        run(p2["ret"])
        forced.append(1); run(p1["v"])
        if nA is not None:
            forced.append(2); run(a2)
        forced.append(0); run(p2["mix"])
        run(a2)
        run(p1["v"])
        run(p1["k"]); run(p1["k"]); run(p1["k"])
        if idx == NT:
            sample_load(0)
            sample_load(1)
            sample_pool_state_load()
        forced.append(1); run(p1["gr"])
        forced.append(0); run(p2["out"])
        run(p2["out"])
        forced.append(3); run(p1["gp"])
        assert not forced
    pj[0] = 0
    P2_sample()

    P.finish()
    P.emit()
    P.stack.close()
    return nc


def _constants():
    c = {}
    c["c_ident"] = np.eye(128, dtype=np.float32)
    half = HD // 2
    inv = 10000.0 ** (-np.arange(half, dtype=np.float64) / half)
    pos = np.zeros((NT + 2, 128), np.float64)
    pos[0, :NMETA] = np.arange(NMETA)
    for t in range(1, NT + 1):
        pos[t] = NMETA + (t - 1) * 128 + np.arange(128)
    pos[NT + 1, :] = PAST
    ang = (pos.astype(np.float32)[:, :, None] * inv.astype(np.float32)[None, None, :]).astype(np.float32).astype(np.float64)
    cs = np.zeros((NT + 2, 128, 192), np.float32)
    cs[:, :, 0:64] = np.cos(ang)
    cs[:, :, 64:128] = np.sin(ang)
    cs[:, :, 128:192] = -np.sin(ang)
    c["c_cs"] = cs
    g = np.array(GAM, np.float64)
    l = np.arange(128, dtype=np.float64)[:, None]
    sc = HD ** -0.5
    dec = np.zeros((128, 6, 8), np.float64)
    dec[:, 0, :] = g[None, :] ** (l + 1.0)
    dec[:, 1, :] = g[None, :] ** (127.0 - l) * sc
    dec[:, 2, :] = g[None, :] ** (l + 1.0)
    dec[:NMETA, 3, :] = g[None, :] ** (15.0 - l[:NMETA]) * sc
    dec[:, 4, :] = g[None, :]
    dec[:, 5, :] = sc
    c["c_dec"] = dec.reshape(128, 48).astype(np.float32)
    m = np.zeros((128, 256), np.float32)
    mi = np.arange(128)
    m[:, 0:128] = (mi[None, :] >= mi[:, None]).astype(np.float32)
    m[:, 128:256] = np.eye(128, dtype=np.float32)
    c["c_mask"] = m
    band = np.zeros((128, 3, 4, 128), np.float32)
    tp = np.arange(128)[:, None]
    t = np.arange(128)[None, :]
    for gi, w in enumerate(WINS):
        d = t - tp
        band[:, 0, gi, :] = ((d >= 0) & (d <= w - 1)) / w - (d == 0)
        d2 = t + 128 - tp
        band[:, 1, gi, :] = (d2 <= w - 1) / w
        d3 = t + NMETA - tp[:NMETA]
        band[:NMETA, 2, gi, :] = (d3 <= w - 1) / w
    c["c_band"] = band.reshape(128, -1).astype(np.float32)
    csel = np.zeros((128, 2, 4, 16), np.float32)
    cu = np.zeros((128, 4, 16), np.float32)
    for gi, w in enumerate(WINS):
        for a in range(2):
            for r in range(120):
                s, j = a * 8 + r // 15, r % 15
                if j >= 15 - (w - 1):
                    csel[r, a, gi, s] = 1.0 / w
        for s in range(16):
            cu[s, gi, s] = 1.0 / w - 1.0
    c["c_csel"] = csel.reshape(128, -1)
    c["c_cu"] = cu.reshape(128, -1)
    oh = np.zeros((128, 16, 16), np.float32)
    oh[:, np.arange(16), np.arange(16)] = 1.0
    oh2 = oh.reshape(128, 256).copy()
    c["c_oh"] = oh2
    return c


_CACHE = {}


def kernel(x_prompt, x_sample, state_ret, state_pool, meta_tokens, norm_w, w_in, w_pool,
           pool_scale, ret_norm_w, w_out, final_norm_w):
    f = lambda a: np.ascontiguousarray(np.asarray(a, dtype=np.float32))
    x_prompt, x_sample, state_ret, state_pool = f(x_prompt), f(x_sample), f(state_ret), f(state_pool)
    if "nc" not in _CACHE:
        _CACHE["nc"] = build_program()
        _CACHE["consts"] = _constants()
    nc = _CACHE["nc"]
    consts = _CACHE["consts"]
    shared = {
        "meta": f(meta_tokens),
        "normw_col": f(np.asarray(norm_w, np.float32).reshape(8, 128).T),
        "w_in": f(np.asarray(w_in)[0]),
        "w_pool": f(np.asarray(w_pool)[0]),
        "pool_scale": f(np.asarray(pool_scale).reshape(1, D)),
        "retw_col": f(np.asarray(ret_norm_w, np.float32).reshape(8, 128).T),
        "w_out": f(np.asarray(w_out)[0]),
        "finw": f(np.asarray(final_norm_w).reshape(1, D)),
    }
    shared.update(consts)
    in_maps = []
    for c in range(8):
        m = dict(shared)
        m["x"] = x_prompt[c]
        m["xs"] = f(x_sample[c * NS:(c + 1) * NS, 0, :])
        m["S_in"] = f(state_ret[0, c * NS:(c + 1) * NS].transpose(1, 2, 0, 3))
        m["pool_in"] = f(state_pool[0, c * NS:(c + 1) * NS].reshape(NS * 15, D))
        in_maps.append(m)
    res = run_bass_kernel_spmd(nc, in_maps, core_ids=list(range(8)))
    R = res.results
    y_prompt = np.stack([R[c]["y"] for c in range(8)], 0).astype(np.float32)
    y_sample = np.concatenate([R[c]["ys"] for c in range(8)], 0).reshape(8 * NS, 1, D).astype(np.float32)
    ret_p = np.stack([R[c]["Sp"] for c in range(8)], 0)[None].astype(np.float32)
    ret_s = np.concatenate([np.asarray(R[c]["Ss"]).transpose(2, 0, 1, 3) for c in range(8)], 0)[None].astype(np.float32)
    pb_p = np.stack([R[c]["pbp"] for c in range(8)], 0)[None].astype(np.float32)
    pb_s = np.concatenate([R[c]["pbs"] for c in range(8)], 0)[None].astype(np.float32)
    return (y_prompt, y_sample, ret_p, ret_s, pb_p, pb_s)
```

```python
from contextlib import ExitStack
import numpy as np
import concourse.bass as bass
import concourse.mybir as mybir
from concourse.bass_utils import run_bass_kernel_spmd

F32 = mybir.dt.float32
BF16 = mybir.dt.bfloat16
ALU = mybir.AluOpType
AF = mybir.ActivationFunctionType

D = 1024
NH = 8
HD = 128
SEQ = 2048
NMETA = 16
NS = 16
PAST = 16384
EPS = 1e-6
NT = SEQ // 128
ENGS = ["pe", "act", "dve", "pool", "sp"]
WINS = (2, 4, 8, 16)
GAM = [1.0 - 2.0 ** (-5.0 - h) for h in range(NH)]


class Prog:
    def __init__(self, nc):
        self.nc = nc
        self.streams = {e: [] for e in ENGS}
        self.cnt = {}
        self.waited = {e: {} for e in ENGS}
        self.last_w = {}
        self.readers = {}
        self.stack = ExitStack()
        self.sems = {}

    def sb(self, name, shape, dt):
        return self.stack.enter_context(self.nc.sbuf_tensor(name, list(shape), dt))

    def ps(self, name, shape, dt):
        return self.stack.enter_context(self.nc.psum_tensor(name, list(shape), dt))

    def op(self, eng, fn, reads=(), writes=(), sem=None):
        semkey = sem or eng
        inc = 16 if sem is not None else 1
        deps = {}

        def add(ev):
            if ev is not None and deps.get(ev[0], 0) < ev[1]:
                deps[ev[0]] = ev[1]

        for b in reads:
            add(self.last_w.get(b))
        for b in writes:
            add(self.last_w.get(b))
            for ev in self.readers.get(b, ()):
                add(ev)
        w = self.waited[eng]
        for k, v in deps.items():
            if w.get(k, 0) < v:
                self.streams[eng].append(("wait", k, v))
                w[k] = v
        newv = self.cnt.get(semkey, 0) + inc
        self.cnt[semkey] = newv
        self.streams[eng].append(("ins", fn, semkey, inc))
        ev = (semkey, newv)
        for b in reads:
            self.readers.setdefault(b, []).append(ev)
        for b in writes:
            self.last_w[b] = ev
            self.readers[b] = []
        return ev

    def finish(self):
        for k, v in self.cnt.items():
            if self.waited["sp"].get(k, 0) < v:
                self.streams["sp"].append(("wait", k, v))
                self.waited["sp"][k] = v

    def emit(self):
        nc = self.nc
        for k in self.cnt:
            self.sems[k] = self.stack.enter_context(nc.semaphore("s_" + k))
        block = self.stack.enter_context(nc.Block())

        def run(e, name):
            for item in self.streams[name]:
                if item[0] == "wait":
                    e.wait_ge(self.sems[item[1]], item[2])
                else:
                    _, fn, semkey, inc = item
                    fn(e).then_inc(self.sems[semkey], inc)

        @block.tensor
        def _(e):
            run(e, "pe")

        @block.scalar
        def _(e):
            run(e, "act")

        @block.vector
        def _(e):
            run(e, "dve")

        @block.gpsimd
        def _(e):
            run(e, "pool")

        @block.sync
        def _(e):
            run(e, "sp")


def build_program():
    nc = bass.Bass("TRN2", target_bir_lowering=False)

    def din(name, shape):
        return nc.dram_tensor(name, list(shape), F32, kind="ExternalInput").ap()

    def dout(name, shape):
        return nc.dram_tensor(name, list(shape), F32, kind="ExternalOutput").ap()

    x = din("x", [SEQ, D])
    meta = din("meta", [NMETA, D])
    xs = din("xs", [NS, D])
    S_in = din("S_in", [NH, HD, NS, HD])
    pool_in = din("pool_in", [NS * 15, D])
    normw_col = din("normw_col", [128, 8])
    w_in = din("w_in", [D, 6 * D])
    w_pool = din("w_pool", [4, 256, 256])
    pool_scale = din("pool_scale", [1, D])
    retw_col = din("retw_col", [128, 8])
    w_out = din("w_out", [2 * D, D])
    finw = din("finw", [1, D])
    c_ident = din("c_ident", [128, 128])
    c_cs = din("c_cs", [NT + 2, 128, 192])
    c_dec = din("c_dec", [128, 48])
    c_mask = din("c_mask", [128, 256])
    c_band = din("c_band", [128, 3 * 4 * 128])
    c_csel = din("c_csel", [128, 2 * 4 * 16])
    c_cu = din("c_cu", [128, 4 * 16])
    c_oh = din("c_oh", [128, 256])

    y = dout("y", [SEQ, D])
    ys = dout("ys", [NS, D])
    Sp = dout("Sp", [NH, HD, HD])
    Ss = dout("Ss", [NH, HD, NS, HD])
    pbp = dout("pbp", [15, D])
    pbs = dout("pbs", [NS, 15, D])

    P = Prog(nc)
    w_in_bf = P.sb("w_in_bf", [128, 8, 6 * D], BF16)
    w_out_bf = P.sb("w_out_bf", [128, 16, D], BF16)
    w_pool_bf = P.sb("w_pool_bf", [128, 2048], BF16)
    ident = P.sb("ident", [128, 128], BF16)
    normw = P.sb("normw", [128, 8], F32)
    retwc = P.sb("retwc", [128, 8], F32)
    finw_b = P.sb("finw_b", [128, D], F32)
    dec = P.sb("dec", [128, 48], F32)
    mask = P.sb("mask", [128, 256], F32)
    band = P.sb("band", [128, 3 * 4 * 128], BF16)
    csel = P.sb("csel", [128, 128], BF16)
    cu = P.sb("cu", [128, 64], BF16)
    oh = P.sb("oh", [128, 256], BF16)
    nh = P.sb("nh", [128, 8], F32)
    cs = [P.sb("cs%d" % i, [128, 192], F32) for i in range(2)]
    xall = P.sb("xall", [128, 3 * D], F32)
    xb = [xall[:, i * D:(i + 1) * D] for i in range(3)]
    Fb = P.sb("Fb", [128, 2 * D], F32)
    F = [Fb[:, i * D:(i + 1) * D] for i in range(2)]
    hn = P.sb("hn", [128, D], BF16)
    W0 = P.sb("W0", [128, D], BF16)
    hnTall = P.sb("hnTall", [128, 2 * D], BF16)
    hnTs = [hnTall[:, i * D:(i + 1) * D] for i in range(2)]
    uall = P.sb("uall", [128, 3 * D], BF16)
    u_bf = [uall[:, i * D:(i + 1) * D] for i in range(3)]
    qd = P.sb("qd", [128, D], BF16)
    kd = P.sb("kd", [128, D], BF16)
    qkT = P.sb("qkT", [128, 2 * D], BF16)
    qT = qkT[:, 0:D]
    kT = qkT[:, D:2 * D]
    v_bf = P.sb("v_bf", [128, D], BF16)
    sgpT = P.sb("sgpT", [128, D], BF16)
    G = P.sb("G", [128, D], BF16)
    ret_y = P.sb("ret_y", [128, D], BF16)
    yT = P.sb("yT", [128, 2 * D], BF16)
    S_f32 = P.sb("S_f32", [128, D], F32)
    S_bf = P.sb("S_bf", [128, D], BF16)
    ss_x = P.sb("ss_x", [128, 1], F32)
    rs_x = P.sb("rs_x", [128, 1], F32)
    ss_r = P.sb("ss_r", [128, 8], F32)
    rs_r = P.sb("rs_r", [128, 8], F32)
    ss_f = P.sb("ss_f", [128, 1], F32)
    rs_f = P.sb("rs_f", [128, 1], F32)
    pp = [P.ps("pp%d" % i, [128, D], F32) for i in range(4)]
    ppb = [t[:].bitcast(BF16) for t in pp]
    pj = [0]

    forced = []

    def next_pair():
        if forced:
            i = forced.pop(0)
        else:
            i = pj[0] % 4
            pj[0] += 1
        return pp[i], ppb[i], "pp%d" % i

    uniq = [0]

    def dma(eng, out, in_, reads, writes, sem=None):
        if sem is None:
            uniq[0] += 1
            sem = "d_u%d" % uniq[0]
        P.op(eng, lambda e: e.dma_start(out=out, in_=in_), reads=reads, writes=writes, sem=sem)

    dma("pool", ident[:], c_ident[:, :], [], ["ident"])
    dma("sp", normw[:], normw_col[:, :], [], ["normw"])
    dma("sp", retwc[:], retw_col[:, :], [], ["retwc"])
    dma("sp", dec[:], c_dec[:, :], [], ["dec"])
    dma("sp", mask[:], c_mask[:, :], [], ["mask"])
    P.op("dve", lambda e: e.memset(nh[:], -0.5), writes=["nh"])
    w_in_v = w_in.rearrange("(kc p) n -> p kc n", p=128)
    for blk in (0, 1, 6, 7, 8, 9, 4, 5, 10, 11, 2, 3):
        dma("pool", w_in_bf[:, :, blk * 512:(blk + 1) * 512], w_in_v[:, :, blk * 512:(blk + 1) * 512],
            [], ["win%d" % blk])
        if blk == 1:
            dma("pool", band[:], c_band[:, :], [], ["band"])
            dma("pool", csel[:], c_csel[:, :], [], ["csel"])
            dma("pool", cu[:], c_cu[:, :], [], ["cu"])
            dma("pool", oh[:], c_oh[:, :], [], ["oh"])
    dma("sp", finw_b[:], finw[0:1, :].broadcast_to([128, D]), [], ["finw_b"])
    dma("sp", Fb[:, 0:2048].rearrange("p (g cc d) -> p g cc d", g=4, cc=2),
        w_pool.rearrange("g (cc p) d -> p g cc d", p=128), [], ["F0", "F1"])
    dma("sp", xb[2], pool_scale[0:1, :].broadcast_to([128, D]), [], ["xb2"])
    P.op("dve", lambda e: e.tensor_tensor(
        out=w_pool_bf[:].rearrange("p (g cc d) -> p g cc d", g=4, cc=2),
        in0=Fb[:, 0:2048].rearrange("p (g cc d) -> p g cc d", g=4, cc=2),
        in1=xb[2].rearrange("p (g d) -> p g d", g=4).unsqueeze(2).broadcast_to([128, 4, 2, 256]),
        op=ALU.mult), reads=["F0", "F1", "xb2"], writes=["w_pool_bf"])
    w_out_v = w_out.rearrange("(kc p) n -> p kc n", p=128)

    def load_w_out():
        for blk in range(4):
            dma("pool", w_out_bf[:, blk * 4:(blk + 1) * 4, :], w_out_v[:, blk * 4:(blk + 1) * 4, :],
                [], ["wout%d" % blk])

    def rstd(ss, rs, n, T, inv_n, key_ss, key_rs):
        P.op("dve", lambda e: e.tensor_scalar(out=rs[:T, 0:n], in0=ss[:T, 0:n], scalar1=inv_n, scalar2=EPS,
                                              op0=ALU.mult, op1=ALU.add), reads=[key_ss], writes=[key_rs])
        P.op("pool", lambda e: e.tensor_tensor(out=rs[:T, 0:n], in0=rs[:T, 0:n], in1=nh[:T, 0:n], op=ALU.pow),
             reads=[key_rs, "nh"], writes=[key_rs])

    def act_evac(T, pt, pk, dst, dkey, func=AF.Copy, ncol=D):
        P.op("act", lambda e: e.activation(out=dst[:T, 0:ncol], in_=pt[:T, 0:ncol], func=func),
             reads=[pk], writes=[dkey])

    def dve_copy(rows, ncol, src, skeys, dst, dkeys):
        P.op("dve", lambda e: e.tensor_copy(out=dst[:rows, 0:ncol], in_=src[:rows, 0:ncol]),
             reads=list(skeys), writes=list(dkeys))

    def load_x(T, xsrc, cs_idx, xi, ci):
        dma("sp", xb[xi][:T, :], xsrc, [], ["xb%d" % xi], "d_x%d" % xi)
        dma("sp", cs[ci][:T, :], c_cs[cs_idx, 0:T, :], [], ["cs%d" % ci], "d_cs%d" % ci)

    def stage_A1(T, xi):
        xk = "xb%d" % xi
        P.op("act", lambda e: e.activation(out=hn[:T, :], in_=xb[xi][:T, :], func=AF.Square, accum_out=ss_x[:T, :]),
             reads=[xk], writes=["hn", "ss_x"])
        rstd(ss_x, rs_x, 1, T, 1.0 / D, "ss_x", "rs_x")
        P.op("dve", lambda e: e.tensor_scalar(out=hn[:T, :], in0=xb[xi][:T, :], scalar1=rs_x[:T, 0:1], scalar2=None,
                                              op0=ALU.mult), reads=[xk, "rs_x"], writes=["hn"])

    def stage_A2(T, hi):
        hnT = hnTs[hi]
        pt, ptb, pk = next_pair()

        def tr(e):
            r = None
            for kc in range(8):
                r = e.transpose(out=ptb[:, kc * T:(kc + 1) * T], in_=hn[:T, kc * 128:(kc + 1) * 128],
                                identity=ident[:T, :T])
            return r
        P.op("pe", tr, reads=["hn", "ident"], writes=[pk])
        yield

        def ev(e):
            r = None
            for kc in range(8):
                r = e.tensor_scalar(out=hnT[:, kc * T:(kc + 1) * T], in0=ptb[:, kc * T:(kc + 1) * T],
                                    scalar1=normw[:, kc:kc + 1], scalar2=None, op0=ALU.mult)
            return r
        P.op("dve", ev, reads=[pk, "normw"], writes=["hnT%d" % hi])
        yield

    def proj_A(T, col0, hi):
        hnT = hnTs[hi]
        pt, ptb, pk = next_pair()

        def mm(e):
            r = None
            for nt in range(2):
                for kc in range(8):
                    c0 = col0 + nt * 512
                    r = e.matmul(pt[:T, nt * 512:(nt + 1) * 512], lhsT=hnT[:, kc * T:(kc + 1) * T],
                                 rhs=w_in_bf[:, kc, c0:c0 + 512], start=(kc == 0), stop=(kc == 7))
            return r
        b0 = col0 // 512
        P.op("pe", mm, reads=["hnT%d" % hi, "win%d" % b0, "win%d" % (b0 + 1)], writes=[pk])
        return pt, pk

    def rotary(T, pt, pk, ci, dslot, dst, dkey, fa=0):
        Fa, fak = F[fa], "F%d" % fa

        def evac(e):
            r = None
            for h in range(NH):
                r = e.activation(out=Fa[:T, h * 128:(h + 1) * 128], in_=pt[:T, h * 128:(h + 1) * 128],
                                 func=AF.Copy, scale=dec[:T, dslot * 8 + h:dslot * 8 + h + 1])
            return r
        P.op("act", evac, reads=[pk, "dec"], writes=[fak])
        yield
        f0 = Fa[:T, :].rearrange("p (h two j) -> p h two j", h=8, two=2)
        f1 = F[1][:T, :].rearrange("p (h two j) -> p h two j", h=8, two=2)
        cosb = cs[ci][:T, 0:64].unsqueeze(1).unsqueeze(1).broadcast_to([T, 8, 2, 64])
        sinb = cs[ci][:T, 64:128].unsqueeze(1).broadcast_to([T, 8, 64])
        nsinb = cs[ci][:T, 128:192].unsqueeze(1).broadcast_to([T, 8, 64])

        def rot(e):
            e.tensor_tensor(out=f1[:, :, 0, :], in0=f0[:, :, 1, :], in1=nsinb, op=ALU.mult)
            return e.tensor_tensor(out=f1[:, :, 1, :], in0=f0[:, :, 0, :], in1=sinb, op=ALU.mult)
        P.op("dve", rot, reads=[fak, "cs%d" % ci], writes=["F1"])
        P.op("dve", lambda e: e.tensor_tensor(out=f0, in0=f0, in1=cosb, op=ALU.mult),
             reads=[fak, "F1", "cs%d" % ci], writes=[fak])
        yield
        P.op("dve", lambda e: e.tensor_tensor(out=dst[:T, :], in0=Fa[:T, :], in1=F[1][:T, :], op=ALU.add),
             reads=[fak, "F1"], writes=[dkey])
        yield

    def transposes(T, src, skey, dst, dkey, col0=0, scale_col=None, sckey=None):
        pt, ptb, pk = next_pair()

        def tr(e):
            r = None
            for h in range(NH):
                r = e.transpose(out=ptb[:, h * T:(h + 1) * T], in_=src[:T, h * 128:(h + 1) * 128],
                                identity=ident[:T, :T])
            return r
        P.op("pe", tr, reads=(list(skey) if isinstance(skey, (list, tuple)) else [skey]) + ["ident"], writes=[pk])
        if scale_col is None:
            P.op("act", lambda e: e.activation(out=dst[:, col0:col0 + 8 * T], in_=ptb[:, 0:8 * T], func=AF.Copy),
                 reads=[pk], writes=[dkey])
        else:
            def ev(e):
                r = None
                for h in range(NH):
                    r = e.tensor_scalar(out=dst[:, col0 + h * 128:col0 + h * 128 + T], in0=ptb[:, h * T:(h + 1) * T],
                                        scalar1=scale_col[:, h:h + 1], scalar2=None, op0=ALU.mult)
                return r
            P.op("dve", ev, reads=[pk, sckey], writes=[dkey])

    def out_proj_and_store(T, xi, ydst):
        xk = "xb%d" % xi
        pt, ptb, pk = next_pair()

        def mm(e):
            r = None
            for nt in range(2):
                for kc in range(16):
                    r = e.matmul(pt[:T, nt * 512:(nt + 1) * 512], lhsT=yT[:, kc * 128:kc * 128 + T],
                                 rhs=w_out_bf[:, kc, nt * 512:(nt + 1) * 512], start=(kc == 0), stop=(kc == 15))
            return r
        P.op("pe", mm, reads=["yT", "wout0", "wout1", "wout2", "wout3"], writes=[pk])
        out_post(T, xi, pt, pk, ydst)

    def out_post(T, xi, pt, pk, ydst):
        xk = "xb%d" % xi
        P.op("dve", lambda e: e.tensor_tensor(out=xb[xi][:T, :], in0=xb[xi][:T, :], in1=pt[:T, :], op=ALU.add),
             reads=[xk, pk], writes=[xk])
        P.op("act", lambda e: e.activation(out=ret_y[:T, :], in_=xb[xi][:T, :], func=AF.Square, accum_out=ss_f[:T, :]),
             reads=[xk], writes=["ret_yA", "ret_yB", "ss_f"])
        rstd(ss_f, rs_f, 1, T, 1.0 / D, "ss_f", "rs_f")
        P.op("dve", lambda e: e.scalar_tensor_tensor(out=xb[xi][:T, :], in0=xb[xi][:T, :], scalar=rs_f[:T, 0:1],
                                                     in1=finw_b[:T, :], op0=ALU.mult, op1=ALU.mult),
             reads=[xk, "rs_f", "finw_b"], writes=[xk])
        dma("sp", ydst, xb[xi][:T, :], [xk], [], "d_y%d" % xi)

    def ret_norm_and_gate(T, pr, prk):
        halves = [("A", range(0, 4)), ("B", range(4, 8))]
        for tag, hr in halves:
            def sq(e, hr=hr):
                r = None
                for h in hr:
                    r = e.activation(out=ret_y[:T, h * 128:(h + 1) * 128], in_=pr[:T, h * 128:(h + 1) * 128],
                                     func=AF.Square, accum_out=ss_r[:T, h:h + 1])
                return r
            P.op("act", sq, reads=[prk], writes=["ret_y" + tag, "ss_r" + tag])
        yield
        for tag, hr in halves:
            c0 = hr[0]
            P.op("dve", lambda e, c0=c0: e.tensor_scalar(out=rs_r[:T, c0:c0 + 4], in0=ss_r[:T, c0:c0 + 4],
                                                        scalar1=1.0 / HD, scalar2=EPS, op0=ALU.mult, op1=ALU.add),
                 reads=["ss_r" + tag], writes=["rs_r" + tag])
            P.op("pool", lambda e, c0=c0: e.tensor_tensor(out=rs_r[:T, c0:c0 + 4], in0=rs_r[:T, c0:c0 + 4],
                                                         in1=nh[:T, 0:4], op=ALU.pow),
                 reads=["rs_r" + tag, "nh"], writes=["rs_r" + tag])
        yield
        for tag, hr in halves:
            def gate(e, hr=hr):
                r = None
                for h in hr:
                    r = e.scalar_tensor_tensor(out=ret_y[:T, h * 128:(h + 1) * 128], in0=pr[:T, h * 128:(h + 1) * 128],
                                               scalar=rs_r[:T, h:h + 1], in1=G[:T, h * 128:(h + 1) * 128],
                                               op0=ALU.mult, op1=ALU.mult)
                return r
            P.op("dve", gate, reads=[prk, "rs_r" + tag, "G"], writes=["ret_y" + tag])
        yield

    def ret_yT(T):
        transposes(T, ret_y, ["ret_yA", "ret_yB"], yT, "yT", col0=8 * 128, scale_col=retwc, sckey="retwc")

    def proj_gr(T, hi):
        pt, pk = proj_A(T, 5 * D, hi)
        act_evac(T, pt, pk, G, "G", func=AF.Silu)

    def proj_gp(T, hi):
        hnT = hnTs[hi]
        pt2, ptb2, pk2 = next_pair()

        def mmB(e):
            r = None
            for fc in range(8):
                for kc in range(8):
                    r = e.matmul(pt2[:, fc * T:(fc + 1) * T], lhsT=w_in_bf[:, kc, D + fc * 128:D + (fc + 1) * 128],
                                 rhs=hnT[:, kc * T:(kc + 1) * T], start=(kc == 0), stop=(kc == 7))
            return r
        P.op("pe", mmB, reads=["hnT%d" % hi, "win2", "win3"], writes=[pk2])
        P.op("act", lambda e: e.activation(out=sgpT[:, 0:8 * T], in_=pt2[:, 0:8 * T], func=AF.Silu),
             reads=[pk2], writes=["sgpT"])

    def pooled_evac(T, ppool, pkpool):
        P.op("dve", lambda e: e.tensor_copy(out=W0[:, 0:8 * T], in_=ppool[:, 0:8 * T]), reads=[pkpool], writes=["W0"])

    def mix_and_gate(T, src=None, skey="W0"):
        src = W0 if src is None else src
        pt, ptb, pk = next_pair()

        def mm(e):
            r = None
            for i2 in range(8):
                g, dd = i2 // 2, i2 % 2
                for cc in range(2):
                    o = (g * 2 + cc) * 256 + dd * 128
                    r = e.matmul(pt[:, i2 * T:(i2 + 1) * T], lhsT=w_pool_bf[:, o:o + 128],
                                 rhs=src[:, (2 * g + cc) * T:(2 * g + cc + 1) * T], start=(cc == 0), stop=(cc == 1))
            return r
        P.op("pe", mm, reads=[skey, "w_pool_bf"], writes=[pk])
        P.op("dve", lambda e: e.tensor_tensor(
            out=yT[:, 0:8 * 128].rearrange("p (h t) -> p h t", h=8)[:, :, 0:T],
            in0=pt[:, 0:8 * T].rearrange("p (h t) -> p h t", h=8),
            in1=sgpT[:, 0:8 * T].rearrange("p (h t) -> p h t", h=8), op=ALU.mult),
            reads=[pk, "sgpT"], writes=["yT"])

    def mm_scores(T):
        pt, ptb, pk = next_pair()

        def mm(e):
            r = None
            for h in range(NH):
                r = e.matmul(pt[:T, h * T:(h + 1) * T], lhsT=kT[:, h * T:(h + 1) * T], rhs=qT[:, h * T:(h + 1) * T],
                             start=True, stop=True)
            return r
        P.op("pe", mm, reads=["kT", "qT"], writes=[pk])
        return pt, pk

    def ev_scores(T, pt, pk, scal, mcol0):
        def ev(e):
            r = None
            for h in range(NH):
                r = e.scalar_tensor_tensor(out=W0[:T, h * T:(h + 1) * T], in0=pt[:T, h * T:(h + 1) * T],
                                           scalar=float(scal[h]), in1=mask[:T, mcol0:mcol0 + T],
                                           op0=ALU.mult, op1=ALU.mult)
            return r
        P.op("dve", ev, reads=[pk, "mask"], writes=["W0"])

    def mm_dS(T):
        pt, ptb, pk = next_pair()

        def mm(e):
            r = None
            for h in range(NH):
                r = e.matmul(pt[:, h * 128:(h + 1) * 128], lhsT=kd[:T, h * 128:(h + 1) * 128],
                             rhs=v_bf[:T, h * 128:(h + 1) * 128], start=True, stop=True)
            return r
        P.op("pe", mm, reads=["kd", "v_bf"], writes=[pk])
        return pt, pk

    bandv = band[:].rearrange("p (s g t) -> p s g t", s=3, g=4)

    def tile_T(idx):
        return 128 if 1 <= idx <= NT else 16

    def tile_load(idx):
        T = tile_T(idx)
        if idx == 0:
            src = meta[:, :]
        elif idx <= NT:
            src = x[(idx - 1) * 128:idx * 128, :]
        else:
            src = xs[:, :]
        load_x(T, src, idx, idx % 3, idx % 2)

    def P1_gens(idx):
        T = tile_T(idx)
        ci, ui, hi = idx % 2, idx % 3, idx % 2
        ukey = "u_bf%d" % ui

        def job_u():
            pt, pk = proj_A(T, 0, hi)
            yield
            if idx >= NT:
                act_evac(T, pt, pk, F[1], "F1")
                dve_copy(T, D, F[1], ["F1"], u_bf[ui], [ukey])
                if idx == NT:
                    dma("sp", pbp[:, :], F[1][113:128, :], ["F1"], [])
                else:
                    dma("sp", pbs[:, 14, :], F[1][:T, :], ["F1"], [])
                    dma("sp", pbs[:, 0:14, :], pool_in.rearrange("(s j) d -> s j d", j=15)[:, 1:15, :], [], [])
            else:
                act_evac(T, pt, pk, u_bf[ui], ukey)
            yield

        def job_q():
            pt, pk = proj_A(T, 2 * D, hi)
            yield
            yield from rotary(T, pt, pk, ci, 0 if idx <= NT else 4, qd, "qd")

        def job_k():
            pt, pk = proj_A(T, 3 * D, hi)
            yield
            yield from rotary(T, pt, pk, ci, 3 if idx == 0 else (1 if idx <= NT else 5), kd, "kd")

        def job_v():
            pt, pk = proj_A(T, 4 * D, hi)
            yield
            act_evac(T, pt, pk, v_bf, "v_bf")
            yield

        def job_gr():
            proj_gr(T, hi)
            yield
            yield

        def job_gp():
            proj_gp(T, hi)
            yield
            yield
        d = {"u": job_u(), "k": job_k(), "v": job_v()}
        if idx != 0:
            d.update({"q": job_q(), "gr": job_gr(), "gp": job_gp()})
        return d

    def P2_gens(c):
        T = 128
        xi = c % 3
        ucur, ukey = u_bf[c % 3], "u_bf%d" % (c % 3)
        uprev, upkey = u_bf[(c - 1) % 3], "u_bf%d" % ((c - 1) % 3)

        def j_tr():
            transposes(T, qd, "qd", qT, "qT")
            transposes(T, kd, "kd", kT, "kT")
            yield

        def j_ds_scores():
            pd, pdk = mm_dS(T)
            yield
            pt, pk = mm_scores(T)
            ev_scores(T, pt, pk, [g ** (-128.0) for g in GAM], 0)

            def upd(e):
                r = None
                for h in range(NH):
                    r = e.scalar_tensor_tensor(out=S_f32[:, h * 128:(h + 1) * 128], in0=S_f32[:, h * 128:(h + 1) * 128],
                                               scalar=float(GAM[h] ** 128.0), in1=pd[:, h * 128:(h + 1) * 128],
                                               op0=ALU.mult, op1=ALU.add)
                return r
            P.op("dve", upd, reads=[pdk, "S_f32"], writes=["S_f32"])
            yield

        def j_ret():
            pr, prb, prk = next_pair()

            def mm_ret(e):
                r = None
                for h in range(NH):
                    e.matmul(pr[:T, h * 128:(h + 1) * 128], lhsT=W0[:T, h * T:(h + 1) * T],
                             rhs=v_bf[:T, h * 128:(h + 1) * 128], start=True, stop=False)
                    r = e.matmul(pr[:T, h * 128:(h + 1) * 128], lhsT=qT[:, h * T:(h + 1) * T],
                                 rhs=S_bf[:, h * 128:(h + 1) * 128], start=False, stop=True)
                return r
            P.op("pe", mm_ret, reads=["W0", "v_bf", "qT", "S_bf"], writes=[prk])
            yield
            g = ret_norm_and_gate(T, pr, prk)
            next(g)
            yield
            next(g)
            yield
            next(g)
            yield
            dve_copy(128, D, S_f32, ["S_f32"], S_bf, ["S_bf"])
            if c == NT:
                dma("sp", Sp.rearrange("h d v -> d h v"), S_f32[:].rearrange("p (h v) -> p h v", h=8), ["S_f32"], [])
            yield

        def j_pool():
            pq, pqb, pqk = next_pair()
            Kp, slot = (NMETA, 2) if c == 1 else (128, 1)

            def mm_pool(e):
                r = None
                for j in range(8):
                    g = j // 2
                    e.matmul(pq[:, j * T:(j + 1) * T], lhsT=uprev[:Kp, j * 128:(j + 1) * 128],
                             rhs=bandv[:Kp, slot, g, :], start=True, stop=False)
                    r = e.matmul(pq[:, j * T:(j + 1) * T], lhsT=ucur[:T, j * 128:(j + 1) * 128],
                                 rhs=bandv[:T, 0, g, :], start=False, stop=True)
                return r
            P.op("pe", mm_pool, reads=[ukey, upkey, "band"], writes=[pqk])
            pooled_evac(T, pq, pqk)
            yield

        def j_mix():
            mix_and_gate(T)
            yield
            ret_yT(T)
            yield

        def j_out():
            xk = "xb%d" % xi
            pt, ptb, pk = next_pair()

            def mm(e):
                r = None
                for nt in range(2):
                    for kc in range(16):
                        r = e.matmul(pt[:T, nt * 512:(nt + 1) * 512], lhsT=yT[:, kc * 128:kc * 128 + T],
                                     rhs=w_out_bf[:, kc, nt * 512:(nt + 1) * 512], start=(kc == 0), stop=(kc == 15))
                return r
            P.op("pe", mm, reads=["yT", "wout0", "wout1", "wout2", "wout3"], writes=[pk])
            yield
            out_post(T, xi, pt, pk, y[(c - 1) * 128:c * 128, :])
            yield
        return {"tr": j_tr(), "ds": j_ds_scores(), "ret": j_ret(), "pool": j_pool(), "mix": j_mix(), "out": j_out()}

    def P2_meta():
        pt, pk = mm_dS(NMETA)
        dve_copy(128, D, pt, [pk], S_f32, ["S_f32"])
        dve_copy(128, D, S_f32, ["S_f32"], S_bf, ["S_bf"])

    SinB = [F[0], F[1], S_f32[:, :]]
    sinK = ["F0", "F1", "S_f32"]

    def sample_load(j):
        h, half, b3 = j // 2, j % 2, j % 3
        dma("sp", SinB[b3].rearrange("p (s v) -> p s v", s=8),
            S_in[h, :, 8 * half:8 * half + 8, :], [], [sinK[b3]], "d_sl%d" % b3)

    def sample_pool_state_load():
        spb = S_f32[:].bitcast(BF16)
        pin = pool_in.rearrange("(a r) d -> r a d", a=2)
        dma("pool", spb[:120, 0:2048].rearrange("p (a d) -> p a d", a=2), pin[:, :, :], [], ["S_f32"])

    def P2_sample():
        T = NS
        idx = NT + 1
        xi = idx % 3
        transposes(T, qd, "qd", qT, "qT")
        transposes(T, kd, "kd", kT, "kT")
        pt, pk = mm_scores(T)
        ev_scores(T, pt, pk, [1.0 / g for g in GAM], 128)
        Qm = uall[:, 0:2048]
        Sin, sink = SinB, sinK
        Sout, soutk = [xb[0], xb[1]], ["xb0", "xb1"]
        Sbf, sbk = [hnTs[0], hnTs[1]], ["hnT0", "hnT1"]
        Rh, rhk = [qT, kT], ["qT", "kT"]
        P.op("dve", lambda e: e.tensor_tensor(
            out=Qm.rearrange("p (h s t) -> p h s t", h=8, s=16),
            in0=qT[:, 0:128].rearrange("p (h t) -> p h t", h=8).unsqueeze(2).broadcast_to([128, 8, 16, 16]),
            in1=oh[:, :].rearrange("p (s t) -> p s t", s=16).unsqueeze(1).broadcast_to([128, 8, 16, 16]),
            op=ALU.mult), reads=["qT", "oh"], writes=["u_bf0", "u_bf1"])
        spb = S_f32[:].bitcast(BF16)
        cselv = csel[:].rearrange("p (a g s) -> p a g s", a=2, g=4)
        cuv = cu[:].rearrange("p (g s) -> p g s", g=4)
        us = u_bf[idx % 3]
        ridx = pj[0] % 4
        pr, prb, prk = next_pair()
        pbr = (ridx + 3) % 4

        def pool_branch():
            forced.append(pbr)
            pq, pqb, pqk = next_pair()

            def mm_pool_s(e):
                r = None
                for j in range(8):
                    g = j // 2
                    e.matmul(pq[:, j * T:(j + 1) * T], lhsT=spb[:120, j * 128:(j + 1) * 128],
                             rhs=cselv[:120, 0, g, :], start=True, stop=False)
                    e.matmul(pq[:, j * T:(j + 1) * T], lhsT=spb[:120, 1024 + j * 128:1024 + (j + 1) * 128],
                             rhs=cselv[:120, 1, g, :], start=False, stop=False)
                    r = e.matmul(pq[:, j * T:(j + 1) * T], lhsT=us[:T, j * 128:(j + 1) * 128], rhs=cuv[:T, g, :],
                                 start=False, stop=True)
                return r
            P.op("pe", mm_pool_s, reads=["S_f32", "csel", "cu", "u_bf%d" % (idx % 3)], writes=[pqk])
            P.op("dve", lambda e: e.tensor_copy(out=hn[:, 0:8 * T], in_=pq[:, 0:8 * T]), reads=[pqk], writes=["hn"])
            forced.append(pbr)
            mix_and_gate(T, src=hn, skey="hn")
        qmk2 = ["u_bf0", "u_bf1"]
        NJ = 2 * NH

        def cast(j):
            bb, b3 = j % 2, j % 3
            P.op("act", lambda e: e.activation(out=Sbf[bb], in_=Sin[b3], func=AF.Copy),
                 reads=[sink[b3]], writes=[sbk[bb]])

        def ret_s(j):
            h, half, bb = j // 2, j % 2, j % 2

            def mm_ret_s(e):
                if half == 0:
                    e.matmul(pr[:T, h * 128:(h + 1) * 128], lhsT=W0[:T, h * T:(h + 1) * T],
                             rhs=v_bf[:T, h * 128:(h + 1) * 128], start=True, stop=False)
                r = None
                for sl in range(8):
                    s_ = 8 * half + sl
                    r = e.matmul(pr[:T, h * 128:(h + 1) * 128], lhsT=Qm[:, (h * 16 + s_) * 16:(h * 16 + s_ + 1) * 16],
                                 rhs=Sbf[bb][:, sl * 128:(sl + 1) * 128], start=False, stop=(s_ == NS - 1))
                return r
            P.op("pe", mm_ret_s, reads=["W0", "v_bf", sbk[bb]] + qmk2, writes=[prk])

        def rh_build(j):
            h, half, bb = j // 2, j % 2, j % 2
            P.op("dve", lambda e: e.tensor_tensor(
                out=Rh[bb][:T, :].rearrange("p (s v) -> p s v", s=8),
                in0=v_bf[:T, h * 128:(h + 1) * 128].unsqueeze(1).broadcast_to([T, 8, 128]),
                in1=mask[:T, 128 + 8 * half:128 + 8 * half + 8].unsqueeze(2).broadcast_to([T, 8, 128]),
                op=ALU.mult), reads=["v_bf", "mask"], writes=[rhk[bb]])

        def ds_upd(j):
            h, half, bb = j // 2, j % 2, j % 2
            pi = (ridx + 1 + j % 2) % 4
            pa, pak = pp[pi], "pp%d" % pi

            def mm_ds_s(e):
                r = None
                for q2 in range(2):
                    r = e.matmul(pa[:, q2 * 512:(q2 + 1) * 512], lhsT=kd[:T, h * 128:(h + 1) * 128],
                                 rhs=Rh[bb][:T, q2 * 512:(q2 + 1) * 512], start=True, stop=True)
                return r
            P.op("pe", mm_ds_s, reads=["kd", rhk[bb]], writes=[pak])
            b3 = j % 3
            P.op("dve", lambda e: e.scalar_tensor_tensor(
                out=Sout[bb], in0=Sin[b3], scalar=float(GAM[h]), in1=pa[:, :], op0=ALU.mult, op1=ALU.add),
                reads=[pak, sink[b3]], writes=[soutk[bb]])
            dma("pool", Ss[h, :, 8 * half:8 * half + 8, :],
                Sout[bb].rearrange("p (s v) -> p s v", s=8), [soutk[bb]], [], "d_ss%d" % bb)

        pool_branch()
        sample_load(2)
        cast(0)
        ret_s(0)
        rh_build(0)
        for j in range(NJ):
            if j + 1 < NJ:
                rh_build(j + 1)
            ds_upd(j)
            if j + 3 < NJ:
                sample_load(j + 3)
            if j + 1 < NJ:
                cast(j + 1)
                ret_s(j + 1)
        pj[0] = ridx + 1
        for _ in ret_norm_and_gate(T, pr, prk):
            pass
        ret_yT(T)
        out_proj_and_store(T, xi, ys[:, :])

    def run(g):
        next(g, None)

    def full(g):
        for _ in g:
            pass

    tile_load(0)
    tile_load(1)
    stage_A1(tile_T(0), 0)
    full(stage_A2(tile_T(0), 0))
    g0 = P1_gens(0)
    for k in ("u", "k", "v"):
        full(g0[k])
    stage_A1(tile_T(1), 1)
    full(stage_A2(tile_T(1), 1))
    load_w_out()
    P2_meta()
    tile_load(2)
    g1 = P1_gens(1)
    stage_A1(tile_T(2), 2 % 3)
    for k in ("u", "q", "k", "v", "gr", "gp"):
        full(g1[k])
    full(stage_A2(tile_T(2), 2 % 2))
    for idx in range(1, NT + 1):
        nA = idx + 2 if idx + 2 <= NT + 1 else None
        if nA is not None:
            tile_load(nA)
        p1 = P1_gens(idx + 1)
        p2 = P2_gens(idx)
        a2 = stage_A2(tile_T(nA), nA % 2) if nA is not None else iter(())
        pj[0] = 0
        forced.extend([0, 1]); run(p2["tr"])
        forced.append(2); run(p2["ds"])
        forced.append(3); run(p1["u"])
        forced.append(0); run(p2["ds"])
        if nA is not None:
            stage_A1(tile_T(nA), nA % 3)
        run(p1["u"])
        forced.append(1); run(p1["q"])
        forced.append(2); run(p2["ret"])
        forced.append(0); run(p2["pool"])
        run(p2["ret"])
        run(p1["q"])
        run(p2["ret"])
        run(p2["ret"])
        run(p1["q"])
        forced.append(3); run(p1["k"])
        forced.append(0); run(p2["mix"])
        run(p1["q"])
        run(p2["ret"])
        if idx == NT:
            run(p1["k"]); run(p1["k"]); run(p1["k"])
            sample_load(0)
            sample_load(1)
            sample_pool_state_load()
        forced.append(1); run(p1["v"])
        if nA is not None:
            forced.append(2); run(a2)
        forced.append(0); run(p2["mix"])
        run(a2)
        run(p1["v"])
        if idx != NT:
            run(p1["k"]); run(p1["k"]); run(p1["k"])
        forced.append(1); run(p1["gr"])
        forced.append(0); run(p2["out"])
        run(p2["out"])
        forced.append(3); run(p1["gp"])
        assert not forced
    pj[0] = 0
    P2_sample()

    P.finish()
    P.emit()
    P.stack.close()
    return nc


def _constants():
    c = {}
    c["c_ident"] = np.eye(128, dtype=np.float32)
    half = HD // 2
    inv = 10000.0 ** (-np.arange(half, dtype=np.float64) / half)
    pos = np.zeros((NT + 2, 128), np.float64)
    pos[0, :NMETA] = np.arange(NMETA)
    for t in range(1, NT + 1):
        pos[t] = NMETA + (t - 1) * 128 + np.arange(128)
    pos[NT + 1, :] = PAST
    ang = (pos.astype(np.float32)[:, :, None] * inv.astype(np.float32)[None, None, :]).astype(np.float32).astype(np.float64)
    cs = np.zeros((NT + 2, 128, 192), np.float32)
    cs[:, :, 0:64] = np.cos(ang)
    cs[:, :, 64:128] = np.sin(ang)
    cs[:, :, 128:192] = -np.sin(ang)
    c["c_cs"] = cs
    g = np.array(GAM, np.float64)
    l = np.arange(128, dtype=np.float64)[:, None]
    sc = HD ** -0.5
    dec = np.zeros((128, 6, 8), np.float64)
    dec[:, 0, :] = g[None, :] ** (l + 1.0)
    dec[:, 1, :] = g[None, :] ** (127.0 - l) * sc
    dec[:, 2, :] = g[None, :] ** (l + 1.0)
    dec[:NMETA, 3, :] = g[None, :] ** (15.0 - l[:NMETA]) * sc
    dec[:, 4, :] = g[None, :]
    dec[:, 5, :] = sc
    c["c_dec"] = dec.reshape(128, 48).astype(np.float32)
    m = np.zeros((128, 256), np.float32)
    mi = np.arange(128)
    m[:, 0:128] = (mi[None, :] >= mi[:, None]).astype(np.float32)
    m[:, 128:256] = np.eye(128, dtype=np.float32)
    c["c_mask"] = m
    band = np.zeros((128, 3, 4, 128), np.float32)
    tp = np.arange(128)[:, None]
    t = np.arange(128)[None, :]
    for gi, w in enumerate(WINS):
        d = t - tp
        band[:, 0, gi, :] = ((d >= 0) & (d <= w - 1)) / w - (d == 0)
        d2 = t + 128 - tp
        band[:, 1, gi, :] = (d2 <= w - 1) / w
        d3 = t + NMETA - tp[:NMETA]
        band[:NMETA, 2, gi, :] = (d3 <= w - 1) / w
    c["c_band"] = band.reshape(128, -1).astype(np.float32)
    csel = np.zeros((128, 2, 4, 16), np.float32)
    cu = np.zeros((128, 4, 16), np.float32)
    for gi, w in enumerate(WINS):
        for a in range(2):
            for r in range(120):
                s, j = a * 8 + r // 15, r % 15
                if j >= 15 - (w - 1):
                    csel[r, a, gi, s] = 1.0 / w
        for s in range(16):
            cu[s, gi, s] = 1.0 / w - 1.0
    c["c_csel"] = csel.reshape(128, -1)
    c["c_cu"] = cu.reshape(128, -1)
    oh = np.zeros((128, 16, 16), np.float32)
    oh[:, np.arange(16), np.arange(16)] = 1.0
    oh2 = oh.reshape(128, 256).copy()
    c["c_oh"] = oh2
    return c


_CACHE = {}


def kernel(x_prompt, x_sample, state_ret, state_pool, meta_tokens, norm_w, w_in, w_pool,
           pool_scale, ret_norm_w, w_out, final_norm_w):
    f = lambda a: np.ascontiguousarray(np.asarray(a, dtype=np.float32))
    x_prompt, x_sample, state_ret, state_pool = f(x_prompt), f(x_sample), f(state_ret), f(state_pool)
    if "nc" not in _CACHE:
        _CACHE["nc"] = build_program()
        _CACHE["consts"] = _constants()
    nc = _CACHE["nc"]
    consts = _CACHE["consts"]
    shared = {
        "meta": f(meta_tokens),
        "normw_col": f(np.asarray(norm_w, np.float32).reshape(8, 128).T),
        "w_in": f(np.asarray(w_in)[0]),
        "w_pool": f(np.asarray(w_pool)[0]),
        "pool_scale": f(np.asarray(pool_scale).reshape(1, D)),
        "retw_col": f(np.asarray(ret_norm_w, np.float32).reshape(8, 128).T),
        "w_out": f(np.asarray(w_out)[0]),
        "finw": f(np.asarray(final_norm_w).reshape(1, D)),
    }
    shared.update(consts)
    in_maps = []
    for c in range(8):
        m = dict(shared)
        m["x"] = x_prompt[c]
        m["xs"] = f(x_sample[c * NS:(c + 1) * NS, 0, :])
        m["S_in"] = f(state_ret[0, c * NS:(c + 1) * NS].transpose(1, 2, 0, 3))
        m["pool_in"] = f(state_pool[0, c * NS:(c + 1) * NS].reshape(NS * 15, D))
        in_maps.append(m)
    res = run_bass_kernel_spmd(nc, in_maps, core_ids=list(range(8)))
    R = res.results
    y_prompt = np.stack([R[c]["y"] for c in range(8)], 0).astype(np.float32)
    y_sample = np.concatenate([R[c]["ys"] for c in range(8)], 0).reshape(8 * NS, 1, D).astype(np.float32)
    ret_p = np.stack([R[c]["Sp"] for c in range(8)], 0)[None].astype(np.float32)
    ret_s = np.concatenate([np.asarray(R[c]["Ss"]).transpose(2, 0, 1, 3) for c in range(8)], 0)[None].astype(np.float32)
    pb_p = np.stack([R[c]["pbp"] for c in range(8)], 0)[None].astype(np.float32)
    pb_s = np.concatenate([R[c]["pbs"] for c in range(8)], 0)[None].astype(np.float32)
    return (y_prompt, y_sample, ret_p, ret_s, pb_p, pb_s)
```

```python
from contextlib import ExitStack
import numpy as np
import concourse.bass as bass
import concourse.mybir as mybir
from concourse.bass_utils import run_bass_kernel_spmd

F32 = mybir.dt.float32
BF16 = mybir.dt.bfloat16
ALU = mybir.AluOpType
AF = mybir.ActivationFunctionType

D = 1024
NH = 8
HD = 128
SEQ = 2048
NMETA = 16
NS = 16
PAST = 16384
EPS = 1e-6
NT = SEQ // 128
ENGS = ["pe", "act", "dve", "pool", "sp"]
WINS = (2, 4, 8, 16)
GAM = [1.0 - 2.0 ** (-5.0 - h) for h in range(NH)]


class Prog:
    def __init__(self, nc):
        self.nc = nc
        self.streams = {e: [] for e in ENGS}
        self.cnt = {}
        self.waited = {e: {} for e in ENGS}
        self.last_w = {}
        self.readers = {}
        self.stack = ExitStack()
        self.sems = {}

    def sb(self, name, shape, dt):
        return self.stack.enter_context(self.nc.sbuf_tensor(name, list(shape), dt))

    def ps(self, name, shape, dt):
        return self.stack.enter_context(self.nc.psum_tensor(name, list(shape), dt))

    def op(self, eng, fn, reads=(), writes=(), sem=None):
        semkey = sem or eng
        inc = 16 if sem is not None else 1
        deps = {}

        def add(ev):
            if ev is not None and deps.get(ev[0], 0) < ev[1]:
                deps[ev[0]] = ev[1]

        for b in reads:
            add(self.last_w.get(b))
        for b in writes:
            add(self.last_w.get(b))
            for ev in self.readers.get(b, ()):
                add(ev)
        w = self.waited[eng]
        for k, v in deps.items():
            if w.get(k, 0) < v:
                self.streams[eng].append(("wait", k, v))
                w[k] = v
        newv = self.cnt.get(semkey, 0) + inc
        self.cnt[semkey] = newv
        self.streams[eng].append(("ins", fn, semkey, inc))
        ev = (semkey, newv)
        for b in reads:
            self.readers.setdefault(b, []).append(ev)
        for b in writes:
            self.last_w[b] = ev
            self.readers[b] = []
        return ev

    def finish(self):
        for k, v in self.cnt.items():
            if self.waited["sp"].get(k, 0) < v:
                self.streams["sp"].append(("wait", k, v))
                self.waited["sp"][k] = v

    def emit(self):
        nc = self.nc
        for k in self.cnt:
            self.sems[k] = self.stack.enter_context(nc.semaphore("s_" + k))
        block = self.stack.enter_context(nc.Block())

        def run(e, name):
            for item in self.streams[name]:
                if item[0] == "wait":
                    e.wait_ge(self.sems[item[1]], item[2])
                else:
                    _, fn, semkey, inc = item
                    fn(e).then_inc(self.sems[semkey], inc)

        @block.tensor
        def _(e):
            run(e, "pe")

        @block.scalar
        def _(e):
            run(e, "act")

        @block.vector
        def _(e):
            run(e, "dve")

        @block.gpsimd
        def _(e):
            run(e, "pool")

        @block.sync
        def _(e):
            run(e, "sp")


def build_program():
    nc = bass.Bass("TRN2", target_bir_lowering=False)

    def din(name, shape):
        return nc.dram_tensor(name, list(shape), F32, kind="ExternalInput").ap()

    def dout(name, shape):
        return nc.dram_tensor(name, list(shape), F32, kind="ExternalOutput").ap()

    x = din("x", [SEQ, D])
    meta = din("meta", [NMETA, D])
    xs = din("xs", [NS, D])
    S_in = din("S_in", [NH, HD, NS, HD])
    pool_in = din("pool_in", [NS * 15, D])
    normw_col = din("normw_col", [128, 8])
    w_in = din("w_in", [D, 6 * D])
    w_pool = din("w_pool", [4, 256, 256])
    pool_scale = din("pool_scale", [1, D])
    retw_col = din("retw_col", [128, 8])
    w_out = din("w_out", [2 * D, D])
    finw = din("finw", [1, D])
    c_ident = din("c_ident", [128, 128])
    c_cs = din("c_cs", [NT + 2, 128, 192])
    c_dec = din("c_dec", [128, 48])
    c_mask = din("c_mask", [128, 256])
    c_band = din("c_band", [128, 3 * 4 * 128])
    c_csel = din("c_csel", [128, 2 * 4 * 16])
    c_cu = din("c_cu", [128, 4 * 16])
    c_oh = din("c_oh", [128, 256])

    y = dout("y", [SEQ, D])
    ys = dout("ys", [NS, D])
    Sp = dout("Sp", [NH, HD, HD])
    Ss = dout("Ss", [NH, HD, NS, HD])
    pbp = dout("pbp", [15, D])
    pbs = dout("pbs", [NS, 15, D])

    P = Prog(nc)
    w_in_bf = P.sb("w_in_bf", [128, 8, 6 * D], BF16)
    w_out_bf = P.sb("w_out_bf", [128, 16, D], BF16)
    w_pool_bf = P.sb("w_pool_bf", [128, 2048], BF16)
    ident = P.sb("ident", [128, 128], BF16)
    normw = P.sb("normw", [128, 8], F32)
    retwc = P.sb("retwc", [128, 8], F32)
    finw_b = P.sb("finw_b", [128, D], F32)
    dec = P.sb("dec", [128, 48], F32)
    mask = P.sb("mask", [128, 256], F32)
    band = P.sb("band", [128, 3 * 4 * 128], BF16)
    csel = P.sb("csel", [128, 128], BF16)
    cu = P.sb("cu", [128, 64], BF16)
    oh = P.sb("oh", [128, 256], BF16)
    nh = P.sb("nh", [128, 8], F32)
    cs = [P.sb("cs%d" % i, [128, 192], F32) for i in range(2)]
    xall = P.sb("xall", [128, 3 * D], F32)
    xb = [xall[:, i * D:(i + 1) * D] for i in range(3)]
    Fb = P.sb("Fb", [128, 2 * D], F32)
    F = [Fb[:, i * D:(i + 1) * D] for i in range(2)]
    hn = P.sb("hn", [128, D], BF16)
    W0 = P.sb("W0", [128, D], BF16)
    hnTall = P.sb("hnTall", [128, 2 * D], BF16)
    hnTs = [hnTall[:, i * D:(i + 1) * D] for i in range(2)]
    uall = P.sb("uall", [128, 3 * D], BF16)
    u_bf = [uall[:, i * D:(i + 1) * D] for i in range(3)]
    qd = P.sb("qd", [128, D], BF16)
    kd = P.sb("kd", [128, D], BF16)
    qkT = P.sb("qkT", [128, 2 * D], BF16)
    qT = qkT[:, 0:D]
    kT = qkT[:, D:2 * D]
    v_bf = P.sb("v_bf", [128, D], BF16)
    sgpT = P.sb("sgpT", [128, D], BF16)
    G = P.sb("G", [128, D], BF16)
    ret_y = P.sb("ret_y", [128, D], BF16)
    yT = P.sb("yT", [128, 2 * D], BF16)
    S_f32 = P.sb("S_f32", [128, D], F32)
    S_bf = P.sb("S_bf", [128, D], BF16)
    ss_x = P.sb("ss_x", [128, 1], F32)
    rs_x = P.sb("rs_x", [128, 1], F32)
    ss_r = P.sb("ss_r", [128, 8], F32)
    rs_r = P.sb("rs_r", [128, 8], F32)
    ss_f = P.sb("ss_f", [128, 1], F32)
    rs_f = P.sb("rs_f", [128, 1], F32)
    pp = [P.ps("pp%d" % i, [128, D], F32) for i in range(4)]
    ppb = [t[:].bitcast(BF16) for t in pp]
    pj = [0]

    forced = []

    def next_pair():
        if forced:
            i = forced.pop(0)
        else:
            i = pj[0] % 4
            pj[0] += 1
        return pp[i], ppb[i], "pp%d" % i

    uniq = [0]

    def dma(eng, out, in_, reads, writes, sem=None):
        if sem is None:
            uniq[0] += 1
            sem = "d_u%d" % uniq[0]
        P.op(eng, lambda e: e.dma_start(out=out, in_=in_), reads=reads, writes=writes, sem=sem)

    dma("pool", ident[:], c_ident[:, :], [], ["ident"])
    dma("sp", normw[:], normw_col[:, :], [], ["normw"])
    dma("sp", retwc[:], retw_col[:, :], [], ["retwc"])
    dma("sp", dec[:], c_dec[:, :], [], ["dec"])
    dma("sp", mask[:], c_mask[:, :], [], ["mask"])
    P.op("dve", lambda e: e.memset(nh[:], -0.5), writes=["nh"])
    w_in_v = w_in.rearrange("(kc p) n -> p kc n", p=128)
    for blk in (0, 1, 6, 7, 8, 9, 4, 5, 10, 11, 2, 3):
        dma("pool", w_in_bf[:, :, blk * 512:(blk + 1) * 512], w_in_v[:, :, blk * 512:(blk + 1) * 512],
            [], ["win%d" % blk])
        if blk == 1:
            dma("pool", band[:], c_band[:, :], [], ["band"])
            dma("pool", csel[:], c_csel[:, :], [], ["csel"])
            dma("pool", cu[:], c_cu[:, :], [], ["cu"])
            dma("pool", oh[:], c_oh[:, :], [], ["oh"])
    dma("sp", finw_b[:], finw[0:1, :].broadcast_to([128, D]), [], ["finw_b"])
    dma("sp", Fb[:, 0:2048].rearrange("p (g cc d) -> p g cc d", g=4, cc=2),
        w_pool.rearrange("g (cc p) d -> p g cc d", p=128), [], ["F0", "F1"])
    dma("sp", xb[2], pool_scale[0:1, :].broadcast_to([128, D]), [], ["xb2"])
    P.op("dve", lambda e: e.tensor_tensor(
        out=w_pool_bf[:].rearrange("p (g cc d) -> p g cc d", g=4, cc=2),
        in0=Fb[:, 0:2048].rearrange("p (g cc d) -> p g cc d", g=4, cc=2),
        in1=xb[2].rearrange("p (g d) -> p g d", g=4).unsqueeze(2).broadcast_to([128, 4, 2, 256]),
        op=ALU.mult), reads=["F0", "F1", "xb2"], writes=["w_pool_bf"])
    w_out_v = w_out.rearrange("(kc p) n -> p kc n", p=128)

    def load_w_out():
        for blk in range(4):
            dma("pool", w_out_bf[:, blk * 4:(blk + 1) * 4, :], w_out_v[:, blk * 4:(blk + 1) * 4, :],
                [], ["wout%d" % blk])

    def rstd(ss, rs, n, T, inv_n, key_ss, key_rs):
        P.op("dve", lambda e: e.tensor_scalar(out=rs[:T, 0:n], in0=ss[:T, 0:n], scalar1=inv_n, scalar2=EPS,
                                              op0=ALU.mult, op1=ALU.add), reads=[key_ss], writes=[key_rs])
        P.op("pool", lambda e: e.tensor_tensor(out=rs[:T, 0:n], in0=rs[:T, 0:n], in1=nh[:T, 0:n], op=ALU.pow),
             reads=[key_rs, "nh"], writes=[key_rs])

    def act_evac(T, pt, pk, dst, dkey, func=AF.Copy, ncol=D):
        P.op("act", lambda e: e.activation(out=dst[:T, 0:ncol], in_=pt[:T, 0:ncol], func=func),
             reads=[pk], writes=[dkey])

    def dve_copy(rows, ncol, src, skeys, dst, dkeys):
        P.op("dve", lambda e: e.tensor_copy(out=dst[:rows, 0:ncol], in_=src[:rows, 0:ncol]),
             reads=list(skeys), writes=list(dkeys))

    def load_x(T, xsrc, cs_idx, xi, ci):
        dma("sp", xb[xi][:T, :], xsrc, [], ["xb%d" % xi], "d_x%d" % xi)
        dma("sp", cs[ci][:T, :], c_cs[cs_idx, 0:T, :], [], ["cs%d" % ci], "d_cs%d" % ci)

    def stage_A1(T, xi):
        xk = "xb%d" % xi
        P.op("act", lambda e: e.activation(out=hn[:T, :], in_=xb[xi][:T, :], func=AF.Square, accum_out=ss_x[:T, :]),
             reads=[xk], writes=["hn", "ss_x"])
        rstd(ss_x, rs_x, 1, T, 1.0 / D, "ss_x", "rs_x")
        P.op("dve", lambda e: e.tensor_scalar(out=hn[:T, :], in0=xb[xi][:T, :], scalar1=rs_x[:T, 0:1], scalar2=None,
                                              op0=ALU.mult), reads=[xk, "rs_x"], writes=["hn"])

    def stage_A2(T, hi):
        hnT = hnTs[hi]
        pt, ptb, pk = next_pair()

        def tr(e):
            r = None
            for kc in range(8):
                r = e.transpose(out=ptb[:, kc * T:(kc + 1) * T], in_=hn[:T, kc * 128:(kc + 1) * 128],
                                identity=ident[:T, :T])
            return r
        P.op("pe", tr, reads=["hn", "ident"], writes=[pk])
        yield

        def ev(e):
            r = None
            for kc in range(8):
                r = e.tensor_scalar(out=hnT[:, kc * T:(kc + 1) * T], in0=ptb[:, kc * T:(kc + 1) * T],
                                    scalar1=normw[:, kc:kc + 1], scalar2=None, op0=ALU.mult)
            return r
        P.op("dve", ev, reads=[pk, "normw"], writes=["hnT%d" % hi])
        yield

    def proj_A(T, col0, hi):
        hnT = hnTs[hi]
        pt, ptb, pk = next_pair()

        def mm(e):
            r = None
            for nt in range(2):
                for kc in range(8):
                    c0 = col0 + nt * 512
                    r = e.matmul(pt[:T, nt * 512:(nt + 1) * 512], lhsT=hnT[:, kc * T:(kc + 1) * T],
                                 rhs=w_in_bf[:, kc, c0:c0 + 512], start=(kc == 0), stop=(kc == 7))
            return r
        b0 = col0 // 512
        P.op("pe", mm, reads=["hnT%d" % hi, "win%d" % b0, "win%d" % (b0 + 1)], writes=[pk])
        return pt, pk

    def rotary(T, pt, pk, ci, dslot, dst, dkey, fa=0):
        Fa, fak = F[fa], "F%d" % fa

        def evac(e):
            r = None
            for h in range(NH):
                r = e.activation(out=Fa[:T, h * 128:(h + 1) * 128], in_=pt[:T, h * 128:(h + 1) * 128],
                                 func=AF.Copy, scale=dec[:T, dslot * 8 + h:dslot * 8 + h + 1])
            return r
        P.op("act", evac, reads=[pk, "dec"], writes=[fak])
        yield
        f0 = Fa[:T, :].rearrange("p (h two j) -> p h two j", h=8, two=2)
        f1 = F[1][:T, :].rearrange("p (h two j) -> p h two j", h=8, two=2)
        cosb = cs[ci][:T, 0:64].unsqueeze(1).unsqueeze(1).broadcast_to([T, 8, 2, 64])
        sinb = cs[ci][:T, 64:128].unsqueeze(1).broadcast_to([T, 8, 64])
        nsinb = cs[ci][:T, 128:192].unsqueeze(1).broadcast_to([T, 8, 64])

        def rot(e):
            e.tensor_tensor(out=f1[:, :, 0, :], in0=f0[:, :, 1, :], in1=nsinb, op=ALU.mult)
            return e.tensor_tensor(out=f1[:, :, 1, :], in0=f0[:, :, 0, :], in1=sinb, op=ALU.mult)
        P.op("dve", rot, reads=[fak, "cs%d" % ci], writes=["F1"])
        P.op("dve", lambda e: e.tensor_tensor(out=f0, in0=f0, in1=cosb, op=ALU.mult),
             reads=[fak, "F1", "cs%d" % ci], writes=[fak])
        yield
        P.op("dve", lambda e: e.tensor_tensor(out=dst[:T, :], in0=Fa[:T, :], in1=F[1][:T, :], op=ALU.add),
             reads=[fak, "F1"], writes=[dkey])
        yield

    def transposes(T, src, skey, dst, dkey, col0=0, scale_col=None, sckey=None):
        pt, ptb, pk = next_pair()

        def tr(e):
            r = None
            for h in range(NH):
                r = e.transpose(out=ptb[:, h * T:(h + 1) * T], in_=src[:T, h * 128:(h + 1) * 128],
                                identity=ident[:T, :T])
            return r
        P.op("pe", tr, reads=(list(skey) if isinstance(skey, (list, tuple)) else [skey]) + ["ident"], writes=[pk])
        if scale_col is None:
            P.op("act", lambda e: e.activation(out=dst[:, col0:col0 + 8 * T], in_=ptb[:, 0:8 * T], func=AF.Copy),
                 reads=[pk], writes=[dkey])
        else:
            def ev(e):
                r = None
                for h in range(NH):
                    r = e.tensor_scalar(out=dst[:, col0 + h * 128:col0 + h * 128 + T], in0=ptb[:, h * T:(h + 1) * T],
                                        scalar1=scale_col[:, h:h + 1], scalar2=None, op0=ALU.mult)
                return r
            P.op("dve", ev, reads=[pk, sckey], writes=[dkey])

    def out_proj_and_store(T, xi, ydst):
        xk = "xb%d" % xi
        pt, ptb, pk = next_pair()

        def mm(e):
            r = None
            for nt in range(2):
                for kc in range(16):
                    r = e.matmul(pt[:T, nt * 512:(nt + 1) * 512], lhsT=yT[:, kc * 128:kc * 128 + T],
                                 rhs=w_out_bf[:, kc, nt * 512:(nt + 1) * 512], start=(kc == 0), stop=(kc == 15))
            return r
        P.op("pe", mm, reads=["yT", "wout0", "wout1", "wout2", "wout3"], writes=[pk])
        out_post(T, xi, pt, pk, ydst)

    def out_post(T, xi, pt, pk, ydst):
        xk = "xb%d" % xi
        P.op("dve", lambda e: e.tensor_tensor(out=xb[xi][:T, :], in0=xb[xi][:T, :], in1=pt[:T, :], op=ALU.add),
             reads=[xk, pk], writes=[xk])
        P.op("act", lambda e: e.activation(out=ret_y[:T, :], in_=xb[xi][:T, :], func=AF.Square, accum_out=ss_f[:T, :]),
             reads=[xk], writes=["ret_yA", "ret_yB", "ss_f"])
        rstd(ss_f, rs_f, 1, T, 1.0 / D, "ss_f", "rs_f")
        P.op("dve", lambda e: e.scalar_tensor_tensor(out=xb[xi][:T, :], in0=xb[xi][:T, :], scalar=rs_f[:T, 0:1],
                                                     in1=finw_b[:T, :], op0=ALU.mult, op1=ALU.mult),
             reads=[xk, "rs_f", "finw_b"], writes=[xk])
        dma("sp", ydst, xb[xi][:T, :], [xk], [], "d_y%d" % xi)

    def ret_norm_and_gate(T, pr, prk):
        halves = [("A", range(0, 4)), ("B", range(4, 8))]
        for tag, hr in halves:
            def sq(e, hr=hr):
                r = None
                for h in hr:
                    r = e.activation(out=ret_y[:T, h * 128:(h + 1) * 128], in_=pr[:T, h * 128:(h + 1) * 128],
                                     func=AF.Square, accum_out=ss_r[:T, h:h + 1])
                return r
            P.op("act", sq, reads=[prk], writes=["ret_y" + tag, "ss_r" + tag])
        yield
        for tag, hr in halves:
            c0 = hr[0]
            P.op("dve", lambda e, c0=c0: e.tensor_scalar(out=rs_r[:T, c0:c0 + 4], in0=ss_r[:T, c0:c0 + 4],
                                                        scalar1=1.0 / HD, scalar2=EPS, op0=ALU.mult, op1=ALU.add),
                 reads=["ss_r" + tag], writes=["rs_r" + tag])
            P.op("pool", lambda e, c0=c0: e.tensor_tensor(out=rs_r[:T, c0:c0 + 4], in0=rs_r[:T, c0:c0 + 4],
                                                         in1=nh[:T, 0:4], op=ALU.pow),
                 reads=["rs_r" + tag, "nh"], writes=["rs_r" + tag])
        yield
        for tag, hr in halves:
            def gate(e, hr=hr):
                r = None
                for h in hr:
                    r = e.scalar_tensor_tensor(out=ret_y[:T, h * 128:(h + 1) * 128], in0=pr[:T, h * 128:(h + 1) * 128],
                                               scalar=rs_r[:T, h:h + 1], in1=G[:T, h * 128:(h + 1) * 128],
                                               op0=ALU.mult, op1=ALU.mult)
                return r
            P.op("dve", gate, reads=[prk, "rs_r" + tag, "G"], writes=["ret_y" + tag])
        yield

    def ret_yT(T):
        transposes(T, ret_y, ["ret_yA", "ret_yB"], yT, "yT", col0=8 * 128, scale_col=retwc, sckey="retwc")

    def proj_gr(T, hi):
        pt, pk = proj_A(T, 5 * D, hi)
        act_evac(T, pt, pk, G, "G", func=AF.Silu)

    def proj_gp(T, hi):
        hnT = hnTs[hi]
        pt2, ptb2, pk2 = next_pair()

        def mmB(e):
            r = None
            for fc in range(8):
                for kc in range(8):
                    r = e.matmul(pt2[:, fc * T:(fc + 1) * T], lhsT=w_in_bf[:, kc, D + fc * 128:D + (fc + 1) * 128],
                                 rhs=hnT[:, kc * T:(kc + 1) * T], start=(kc == 0), stop=(kc == 7))
            return r
        P.op("pe", mmB, reads=["hnT%d" % hi, "win2", "win3"], writes=[pk2])
        P.op("act", lambda e: e.activation(out=sgpT[:, 0:8 * T], in_=pt2[:, 0:8 * T], func=AF.Silu),
             reads=[pk2], writes=["sgpT"])

    def pooled_evac(T, ppool, pkpool):
        P.op("dve", lambda e: e.tensor_copy(out=W0[:, 0:8 * T], in_=ppool[:, 0:8 * T]), reads=[pkpool], writes=["W0"])

    def mix_and_gate(T, src=None, skey="W0"):
        src = W0 if src is None else src
        pt, ptb, pk = next_pair()

        def mm(e):
            r = None
            for i2 in range(8):
                g, dd = i2 // 2, i2 % 2
                for cc in range(2):
                    o = (g * 2 + cc) * 256 + dd * 128
                    r = e.matmul(pt[:, i2 * T:(i2 + 1) * T], lhsT=w_pool_bf[:, o:o + 128],
                                 rhs=src[:, (2 * g + cc) * T:(2 * g + cc + 1) * T], start=(cc == 0), stop=(cc == 1))
            return r
        P.op("pe", mm, reads=[skey, "w_pool_bf"], writes=[pk])
        P.op("dve", lambda e: e.tensor_tensor(
            out=yT[:, 0:8 * 128].rearrange("p (h t) -> p h t", h=8)[:, :, 0:T],
            in0=pt[:, 0:8 * T].rearrange("p (h t) -> p h t", h=8),
            in1=sgpT[:, 0:8 * T].rearrange("p (h t) -> p h t", h=8), op=ALU.mult),
            reads=[pk, "sgpT"], writes=["yT"])

    def mm_scores(T):
        pt, ptb, pk = next_pair()

        def mm(e):
            r = None
            for h in range(NH):
                r = e.matmul(pt[:T, h * T:(h + 1) * T], lhsT=kT[:, h * T:(h + 1) * T], rhs=qT[:, h * T:(h + 1) * T],
                             start=True, stop=True)
            return r
        P.op("pe", mm, reads=["kT", "qT"], writes=[pk])
        return pt, pk

    def ev_scores(T, pt, pk, scal, mcol0):
        def ev(e):
            r = None
            for h in range(NH):
                r = e.scalar_tensor_tensor(out=W0[:T, h * T:(h + 1) * T], in0=pt[:T, h * T:(h + 1) * T],
                                           scalar=float(scal[h]), in1=mask[:T, mcol0:mcol0 + T],
                                           op0=ALU.mult, op1=ALU.mult)
            return r
        P.op("dve", ev, reads=[pk, "mask"], writes=["W0"])

    def mm_dS(T):
        pt, ptb, pk = next_pair()

        def mm(e):
            r = None
            for h in range(NH):
                r = e.matmul(pt[:, h * 128:(h + 1) * 128], lhsT=kd[:T, h * 128:(h + 1) * 128],
                             rhs=v_bf[:T, h * 128:(h + 1) * 128], start=True, stop=True)
            return r
        P.op("pe", mm, reads=["kd", "v_bf"], writes=[pk])
        return pt, pk

    bandv = band[:].rearrange("p (s g t) -> p s g t", s=3, g=4)

    def tile_T(idx):
        return 128 if 1 <= idx <= NT else 16

    def tile_load(idx):
        T = tile_T(idx)
        if idx == 0:
            src = meta[:, :]
        elif idx <= NT:
            src = x[(idx - 1) * 128:idx * 128, :]
        else:
            src = xs[:, :]
        load_x(T, src, idx, idx % 3, idx % 2)

    def P1_gens(idx):
        T = tile_T(idx)
        ci, ui, hi = idx % 2, idx % 3, idx % 2
        ukey = "u_bf%d" % ui

        def job_u():
            pt, pk = proj_A(T, 0, hi)
            yield
            if idx >= NT:
                act_evac(T, pt, pk, F[1], "F1")
                dve_copy(T, D, F[1], ["F1"], u_bf[ui], [ukey])
                if idx == NT:
                    dma("sp", pbp[:, :], F[1][113:128, :], ["F1"], [])
                else:
                    dma("sp", pbs[:, 14, :], F[1][:T, :], ["F1"], [])
                    dma("sp", pbs[:, 0:14, :], pool_in.rearrange("(s j) d -> s j d", j=15)[:, 1:15, :], [], [])
            else:
                act_evac(T, pt, pk, u_bf[ui], ukey)
            yield

        def job_q():
            pt, pk = proj_A(T, 2 * D, hi)
            yield
            yield from rotary(T, pt, pk, ci, 0 if idx <= NT else 4, qd, "qd")

        def job_k():
            pt, pk = proj_A(T, 3 * D, hi)
            yield
            yield from rotary(T, pt, pk, ci, 3 if idx == 0 else (1 if idx <= NT else 5), kd, "kd")

        def job_v():
            pt, pk = proj_A(T, 4 * D, hi)
            yield
            act_evac(T, pt, pk, v_bf, "v_bf")
            yield

        def job_gr():
            proj_gr(T, hi)
            yield
            yield

        def job_gp():
            proj_gp(T, hi)
            yield
            yield
        d = {"u": job_u(), "k": job_k(), "v": job_v()}
        if idx != 0:
            d.update({"q": job_q(), "gr": job_gr(), "gp": job_gp()})
        return d

    def P2_gens(c):
        T = 128
        xi = c % 3
        ucur, ukey = u_bf[c % 3], "u_bf%d" % (c % 3)
        uprev, upkey = u_bf[(c - 1) % 3], "u_bf%d" % ((c - 1) % 3)

        def j_tr():
            transposes(T, qd, "qd", qT, "qT")
            transposes(T, kd, "kd", kT, "kT")
            yield

        def j_ds_scores():
            pd, pdk = mm_dS(T)
            yield
            pt, pk = mm_scores(T)
            ev_scores(T, pt, pk, [g ** (-128.0) for g in GAM], 0)

            def upd(e):
                r = None
                for h in range(NH):
                    r = e.scalar_tensor_tensor(out=S_f32[:, h * 128:(h + 1) * 128], in0=S_f32[:, h * 128:(h + 1) * 128],
                                               scalar=float(GAM[h] ** 128.0), in1=pd[:, h * 128:(h + 1) * 128],
                                               op0=ALU.mult, op1=ALU.add)
                return r
            P.op("dve", upd, reads=[pdk, "S_f32"], writes=["S_f32"])
            yield

        def j_ret():
            pr, prb, prk = next_pair()

            def mm_ret(e):
                r = None
                for h in range(NH):
                    e.matmul(pr[:T, h * 128:(h + 1) * 128], lhsT=W0[:T, h * T:(h + 1) * T],
                             rhs=v_bf[:T, h * 128:(h + 1) * 128], start=True, stop=False)
                    r = e.matmul(pr[:T, h * 128:(h + 1) * 128], lhsT=qT[:, h * T:(h + 1) * T],
                                 rhs=S_bf[:, h * 128:(h + 1) * 128], start=False, stop=True)
                return r
            P.op("pe", mm_ret, reads=["W0", "v_bf", "qT", "S_bf"], writes=[prk])
            yield
            g = ret_norm_and_gate(T, pr, prk)
            next(g)
            yield
            next(g)
            yield
            next(g)
            yield
            dve_copy(128, D, S_f32, ["S_f32"], S_bf, ["S_bf"])
            if c == NT:
                dma("sp", Sp.rearrange("h d v -> d h v"), S_f32[:].rearrange("p (h v) -> p h v", h=8), ["S_f32"], [])
            yield

        def j_pool():
            pq, pqb, pqk = next_pair()
            Kp, slot = (NMETA, 2) if c == 1 else (128, 1)

            def mm_pool(e):
                r = None
                for j in range(8):
                    g = j // 2
                    e.matmul(pq[:, j * T:(j + 1) * T], lhsT=uprev[:Kp, j * 128:(j + 1) * 128],
                             rhs=bandv[:Kp, slot, g, :], start=True, stop=False)
                    r = e.matmul(pq[:, j * T:(j + 1) * T], lhsT=ucur[:T, j * 128:(j + 1) * 128],
                                 rhs=bandv[:T, 0, g, :], start=False, stop=True)
                return r
            P.op("pe", mm_pool, reads=[ukey, upkey, "band"], writes=[pqk])
            pooled_evac(T, pq, pqk)
            yield

        def j_mix():
            mix_and_gate(T)
            yield
            ret_yT(T)
            yield

        def j_out():
            xk = "xb%d" % xi
            pt, ptb, pk = next_pair()

            def mm(e):
                r = None
                for nt in range(2):
                    for kc in range(16):
                        r = e.matmul(pt[:T, nt * 512:(nt + 1) * 512], lhsT=yT[:, kc * 128:kc * 128 + T],
                                     rhs=w_out_bf[:, kc, nt * 512:(nt + 1) * 512], start=(kc == 0), stop=(kc == 15))
                return r
            P.op("pe", mm, reads=["yT", "wout0", "wout1", "wout2", "wout3"], writes=[pk])
            yield
            out_post(T, xi, pt, pk, y[(c - 1) * 128:c * 128, :])
            yield
        return {"tr": j_tr(), "ds": j_ds_scores(), "ret": j_ret(), "pool": j_pool(), "mix": j_mix(), "out": j_out()}

    def P2_meta():
        pt, pk = mm_dS(NMETA)
        dve_copy(128, D, pt, [pk], S_f32, ["S_f32"])
        dve_copy(128, D, S_f32, ["S_f32"], S_bf, ["S_bf"])

    SinB = [F[0], F[1], S_f32[:, :]]
    sinK = ["F0", "F1", "S_f32"]

    def sample_load(j):
        h, half, b3 = j // 2, j % 2, j % 3
        dma("sp", SinB[b3].rearrange("p (s v) -> p s v", s=8),
            S_in[h, :, 8 * half:8 * half + 8, :], [], [sinK[b3]], "d_sl%d" % b3)

    def sample_pool_state_load():
        spb = S_f32[:].bitcast(BF16)
        pin = pool_in.rearrange("(a r) d -> r a d", a=2)
        dma("pool", spb[:120, 0:2048].rearrange("p (a d) -> p a d", a=2), pin[:, :, :], [], ["S_f32"])

    def P2_sample():
        T = NS
        idx = NT + 1
        xi = idx % 3
        transposes(T, qd, "qd", qT, "qT")
        transposes(T, kd, "kd", kT, "kT")
        pt, pk = mm_scores(T)
        ev_scores(T, pt, pk, [1.0 / g for g in GAM], 128)
        Qm = uall[:, 0:2048]
        Sin, sink = SinB, sinK
        Sout, soutk = [xb[0], xb[1]], ["xb0", "xb1"]
        Sbf, sbk = [hnTs[0], hnTs[1]], ["hnT0", "hnT1"]
        Rh, rhk = [qT, kT], ["qT", "kT"]
        P.op("dve", lambda e: e.tensor_tensor(
            out=Qm.rearrange("p (h s t) -> p h s t", h=8, s=16),
            in0=qT[:, 0:128].rearrange("p (h t) -> p h t", h=8).unsqueeze(2).broadcast_to([128, 8, 16, 16]),
            in1=oh[:, :].rearrange("p (s t) -> p s t", s=16).unsqueeze(1).broadcast_to([128, 8, 16, 16]),
            op=ALU.mult), reads=["qT", "oh"], writes=["u_bf0", "u_bf1"])
        spb = S_f32[:].bitcast(BF16)
        cselv = csel[:].rearrange("p (a g s) -> p a g s", a=2, g=4)
        cuv = cu[:].rearrange("p (g s) -> p g s", g=4)
        us = u_bf[idx % 3]
        ridx = pj[0] % 4
        pr, prb, prk = next_pair()
        pbr = (ridx + 3) % 4

        def pool_branch():
            forced.append(pbr)
            pq, pqb, pqk = next_pair()

            def mm_pool_s(e):
                r = None
                for j in range(8):
                    g = j // 2
                    e.matmul(pq[:, j * T:(j + 1) * T], lhsT=spb[:120, j * 128:(j + 1) * 128],
                             rhs=cselv[:120, 0, g, :], start=True, stop=False)
                    e.matmul(pq[:, j * T:(j + 1) * T], lhsT=spb[:120, 1024 + j * 128:1024 + (j + 1) * 128],
                             rhs=cselv[:120, 1, g, :], start=False, stop=False)
                    r = e.matmul(pq[:, j * T:(j + 1) * T], lhsT=us[:T, j * 128:(j + 1) * 128], rhs=cuv[:T, g, :],
                                 start=False, stop=True)
                return r
            P.op("pe", mm_pool_s, reads=["S_f32", "csel", "cu", "u_bf%d" % (idx % 3)], writes=[pqk])
            P.op("dve", lambda e: e.tensor_copy(out=hn[:, 0:8 * T], in_=pq[:, 0:8 * T]), reads=[pqk], writes=["hn"])
            forced.append(pbr)
            mix_and_gate(T, src=hn, skey="hn")
        qmk2 = ["u_bf0", "u_bf1"]
        NJ = 2 * NH

        def cast(j):
            bb, b3 = j % 2, j % 3
            P.op("act", lambda e: e.activation(out=Sbf[bb], in_=Sin[b3], func=AF.Copy),
                 reads=[sink[b3]], writes=[sbk[bb]])

        def ret_s(j):
            h, half, bb = j // 2, j % 2, j % 2

            def mm_ret_s(e):
                if half == 0:
                    e.matmul(pr[:T, h * 128:(h + 1) * 128], lhsT=W0[:T, h * T:(h + 1) * T],
                             rhs=v_bf[:T, h * 128:(h + 1) * 128], start=True, stop=False)
                r = None
                for sl in range(8):
                    s_ = 8 * half + sl
                    r = e.matmul(pr[:T, h * 128:(h + 1) * 128], lhsT=Qm[:, (h * 16 + s_) * 16:(h * 16 + s_ + 1) * 16],
                                 rhs=Sbf[bb][:, sl * 128:(sl + 1) * 128], start=False, stop=(s_ == NS - 1))
                return r
            P.op("pe", mm_ret_s, reads=["W0", "v_bf", sbk[bb]] + qmk2, writes=[prk])

        def rh_build(j):
            h, half, bb = j // 2, j % 2, j % 2
            P.op("dve", lambda e: e.tensor_tensor(
                out=Rh[bb][:T, :].rearrange("p (s v) -> p s v", s=8),
                in0=v_bf[:T, h * 128:(h + 1) * 128].unsqueeze(1).broadcast_to([T, 8, 128]),
                in1=mask[:T, 128 + 8 * half:128 + 8 * half + 8].unsqueeze(2).broadcast_to([T, 8, 128]),
                op=ALU.mult), reads=["v_bf", "mask"], writes=[rhk[bb]])

        def ds_upd(j):
            h, half, bb = j // 2, j % 2, j % 2
            pi = (ridx + 1 + j % 2) % 4
            pa, pak = pp[pi], "pp%d" % pi

            def mm_ds_s(e):
                r = None
                for q2 in range(2):
                    r = e.matmul(pa[:, q2 * 512:(q2 + 1) * 512], lhsT=kd[:T, h * 128:(h + 1) * 128],
                                 rhs=Rh[bb][:T, q2 * 512:(q2 + 1) * 512], start=True, stop=True)
                return r
            P.op("pe", mm_ds_s, reads=["kd", rhk[bb]], writes=[pak])
            b3 = j % 3
            P.op("dve", lambda e: e.scalar_tensor_tensor(
                out=Sout[bb], in0=Sin[b3], scalar=float(GAM[h]), in1=pa[:, :], op0=ALU.mult, op1=ALU.add),
                reads=[pak, sink[b3]], writes=[soutk[bb]])
            dma("pool", Ss[h, :, 8 * half:8 * half + 8, :],
                Sout[bb].rearrange("p (s v) -> p s v", s=8), [soutk[bb]], [], "d_ss%d" % bb)

        pool_branch()
        sample_load(2)
        cast(0)
        ret_s(0)
        rh_build(0)
        for j in range(NJ):
            if j + 1 < NJ:
                rh_build(j + 1)
            ds_upd(j)
            if j + 3 < NJ:
                sample_load(j + 3)
            if j + 1 < NJ:
                cast(j + 1)
                ret_s(j + 1)
        pj[0] = ridx + 1
        for _ in ret_norm_and_gate(T, pr, prk):
            pass
        ret_yT(T)
        out_proj_and_store(T, xi, ys[:, :])

    def run(g):
        next(g, None)

    def full(g):
        for _ in g:
            pass

    tile_load(0)
    tile_load(1)
    stage_A1(tile_T(0), 0)
    full(stage_A2(tile_T(0), 0))
    g0 = P1_gens(0)
    for k in ("u", "k", "v"):
        full(g0[k])
    stage_A1(tile_T(1), 1)
    full(stage_A2(tile_T(1), 1))
    load_w_out()
    P2_meta()
    tile_load(2)
    g1 = P1_gens(1)
    stage_A1(tile_T(2), 2 % 3)
    for k in ("u", "q", "k", "v"):
        full(g1[k])
    full(stage_A2(tile_T(2), 2 % 2))
    for idx in range(1, NT + 1):
        nA = idx + 2 if idx + 2 <= NT + 1 else None
        if nA is not None:
            tile_load(nA)
        p1 = P1_gens(idx + 1)
        p2 = P2_gens(idx)
        a2 = stage_A2(tile_T(nA), nA % 2) if nA is not None else iter(())
        pj[0] = 0
        forced.extend([0, 1]); run(p2["tr"])
        forced.append(2); run(p2["ds"])
        forced.append(3); run(p1["u"])
        forced.append(0); run(p2["ds"])
        if nA is not None:
            stage_A1(tile_T(nA), nA % 3)
        run(p1["u"])
        forced.append(1); run(p1["q"])
        forced.append(2); run(p2["ret"])
        forced.append(0); run(p2["pool"])
        if idx == 1:
            forced.append(3); full(g1["gr"])
        run(p2["ret"])
        run(p1["q"])
        run(p2["ret"])
        run(p2["ret"])
        run(p1["q"])
        forced.append(3); run(p1["k"])
        if idx == 1:
            forced.append(1); full(g1["gp"])
        forced.append(0); run(p2["mix"])
        run(p1["q"])
        run(p2["ret"])
        forced.append(1); run(p1["v"])
        if nA is not None:
            forced.append(2); run(a2)
        forced.append(0); run(p2["mix"])
        run(a2)
        run(p1["v"])
        run(p1["k"]); run(p1["k"]); run(p1["k"])
        if idx == NT:
            sample_load(0)
            sample_load(1)
            sample_pool_state_load()
        forced.append(1); run(p1["gr"])
        forced.append(0); run(p2["out"])
        run(p2["out"])
        forced.append(3); run(p1["gp"])
        assert not forced
    pj[0] = 0
    P2_sample()

    P.finish()
    P.emit()
    P.stack.close()
    return nc


def _constants():
    c = {}
    c["c_ident"] = np.eye(128, dtype=np.float32)
    half = HD // 2
    inv = 10000.0 ** (-np.arange(half, dtype=np.float64) / half)
    pos = np.zeros((NT + 2, 128), np.float64)
    pos[0, :NMETA] = np.arange(NMETA)
    for t in range(1, NT + 1):
        pos[t] = NMETA + (t - 1) * 128 + np.arange(128)
    pos[NT + 1, :] = PAST
    ang = (pos.astype(np.float32)[:, :, None] * inv.astype(np.float32)[None, None, :]).astype(np.float32).astype(np.float64)
    cs = np.zeros((NT + 2, 128, 192), np.float32)
    cs[:, :, 0:64] = np.cos(ang)
    cs[:, :, 64:128] = np.sin(ang)
    cs[:, :, 128:192] = -np.sin(ang)
    c["c_cs"] = cs
    g = np.array(GAM, np.float64)
    l = np.arange(128, dtype=np.float64)[:, None]
    sc = HD ** -0.5
    dec = np.zeros((128, 6, 8), np.float64)
    dec[:, 0, :] = g[None, :] ** (l + 1.0)
    dec[:, 1, :] = g[None, :] ** (127.0 - l) * sc
    dec[:, 2, :] = g[None, :] ** (l + 1.0)
    dec[:NMETA, 3, :] = g[None, :] ** (15.0 - l[:NMETA]) * sc
    dec[:, 4, :] = g[None, :]
    dec[:, 5, :] = sc
    c["c_dec"] = dec.reshape(128, 48).astype(np.float32)
    m = np.zeros((128, 256), np.float32)
    mi = np.arange(128)
    m[:, 0:128] = (mi[None, :] >= mi[:, None]).astype(np.float32)
    m[:, 128:256] = np.eye(128, dtype=np.float32)
    c["c_mask"] = m
    band = np.zeros((128, 3, 4, 128), np.float32)
    tp = np.arange(128)[:, None]
    t = np.arange(128)[None, :]
    for gi, w in enumerate(WINS):
        d = t - tp
        band[:, 0, gi, :] = ((d >= 0) & (d <= w - 1)) / w - (d == 0)
        d2 = t + 128 - tp
        band[:, 1, gi, :] = (d2 <= w - 1) / w
        d3 = t + NMETA - tp[:NMETA]
        band[:NMETA, 2, gi, :] = (d3 <= w - 1) / w
    c["c_band"] = band.reshape(128, -1).astype(np.float32)
    csel = np.zeros((128, 2, 4, 16), np.float32)
    cu = np.zeros((128, 4, 16), np.float32)
    for gi, w in enumerate(WINS):
        for a in range(2):
            for r in range(120):
                s, j = a * 8 + r // 15, r % 15
                if j >= 15 - (w - 1):
                    csel[r, a, gi, s] = 1.0 / w
        for s in range(16):
            cu[s, gi, s] = 1.0 / w - 1.0
    c["c_csel"] = csel.reshape(128, -1)
    c["c_cu"] = cu.reshape(128, -1)
    oh = np.zeros((128, 16, 16), np.float32)
    oh[:, np.arange(16), np.arange(16)] = 1.0
    oh2 = oh.reshape(128, 256).copy()
    c["c_oh"] = oh2
    return c


_CACHE = {}


def kernel(x_prompt, x_sample, state_ret, state_pool, meta_tokens, norm_w, w_in, w_pool,
           pool_scale, ret_norm_w, w_out, final_norm_w):
    f = lambda a: np.ascontiguousarray(np.asarray(a, dtype=np.float32))
    x_prompt, x_sample, state_ret, state_pool = f(x_prompt), f(x_sample), f(state_ret), f(state_pool)
    if "nc" not in _CACHE:
        _CACHE["nc"] = build_program()
        _CACHE["consts"] = _constants()
    nc = _CACHE["nc"]
    consts = _CACHE["consts"]
    shared = {
        "meta": f(meta_tokens),
        "normw_col": f(np.asarray(norm_w, np.float32).reshape(8, 128).T),
        "w_in": f(np.asarray(w_in)[0]),
        "w_pool": f(np.asarray(w_pool)[0]),
        "pool_scale": f(np.asarray(pool_scale).reshape(1, D)),
        "retw_col": f(np.asarray(ret_norm_w, np.float32).reshape(8, 128).T),
        "w_out": f(np.asarray(w_out)[0]),
        "finw": f(np.asarray(final_norm_w).reshape(1, D)),
    }
    shared.update(consts)
    in_maps = []
    for c in range(8):
        m = dict(shared)
        m["x"] = x_prompt[c]
        m["xs"] = f(x_sample[c * NS:(c + 1) * NS, 0, :])
        m["S_in"] = f(state_ret[0, c * NS:(c + 1) * NS].transpose(1, 2, 0, 3))
        m["pool_in"] = f(state_pool[0, c * NS:(c + 1) * NS].reshape(NS * 15, D))
        in_maps.append(m)
    res = run_bass_kernel_spmd(nc, in_maps, core_ids=list(range(8)))
    R = res.results
    y_prompt = np.stack([R[c]["y"] for c in range(8)], 0).astype(np.float32)
    y_sample = np.concatenate([R[c]["ys"] for c in range(8)], 0).reshape(8 * NS, 1, D).astype(np.float32)
    ret_p = np.stack([R[c]["Sp"] for c in range(8)], 0)[None].astype(np.float32)
    ret_s = np.concatenate([np.asarray(R[c]["Ss"]).transpose(2, 0, 1, 3) for c in range(8)], 0)[None].astype(np.float32)
    pb_p = np.stack([R[c]["pbp"] for c in range(8)], 0)[None].astype(np.float32)
    pb_s = np.concatenate([R[c]["pbs"] for c in range(8)], 0)[None].astype(np.float32)
    return (y_prompt, y_sample, ret_p, ret_s, pb_p, pb_s)
```

```python
from contextlib import ExitStack
import numpy as np
import concourse.bass as bass
import concourse.mybir as mybir
from concourse.bass_utils import run_bass_kernel_spmd

F32 = mybir.dt.float32
BF16 = mybir.dt.bfloat16
ALU = mybir.AluOpType
AF = mybir.ActivationFunctionType

D = 1024
NH = 8
HD = 128
SEQ = 2048
NMETA = 16
NS = 16
PAST = 16384
EPS = 1e-6
NT = SEQ // 128
ENGS = ["pe", "act", "dve", "pool", "sp"]
WINS = (2, 4, 8, 16)
GAM = [1.0 - 2.0 ** (-5.0 - h) for h in range(NH)]


class Prog:
    def __init__(self, nc):
        self.nc = nc
        self.streams = {e: [] for e in ENGS}
        self.cnt = {}
        self.waited = {e: {} for e in ENGS}
        self.last_w = {}
        self.readers = {}
        self.stack = ExitStack()
        self.sems = {}

    def sb(self, name, shape, dt):
        return self.stack.enter_context(self.nc.sbuf_tensor(name, list(shape), dt))

    def ps(self, name, shape, dt):
        return self.stack.enter_context(self.nc.psum_tensor(name, list(shape), dt))

    def op(self, eng, fn, reads=(), writes=(), sem=None):
        semkey = sem or eng
        inc = 16 if sem is not None else 1
        deps = {}

        def add(ev):
            if ev is not None and deps.get(ev[0], 0) < ev[1]:
                deps[ev[0]] = ev[1]

        for b in reads:
            add(self.last_w.get(b))
        for b in writes:
            add(self.last_w.get(b))
            for ev in self.readers.get(b, ()):
                add(ev)
        w = self.waited[eng]
        for k, v in deps.items():
            if w.get(k, 0) < v:
                self.streams[eng].append(("wait", k, v))
                w[k] = v
        newv = self.cnt.get(semkey, 0) + inc
        self.cnt[semkey] = newv
        self.streams[eng].append(("ins", fn, semkey, inc))
        ev = (semkey, newv)
        for b in reads:
            self.readers.setdefault(b, []).append(ev)
        for b in writes:
            self.last_w[b] = ev
            self.readers[b] = []
        return ev

    def finish(self):
        for k, v in self.cnt.items():
            if self.waited["sp"].get(k, 0) < v:
                self.streams["sp"].append(("wait", k, v))
                self.waited["sp"][k] = v

    def emit(self):
        nc = self.nc
        for k in self.cnt:
            self.sems[k] = self.stack.enter_context(nc.semaphore("s_" + k))
        block = self.stack.enter_context(nc.Block())

        def run(e, name):
            for item in self.streams[name]:
                if item[0] == "wait":
                    e.wait_ge(self.sems[item[1]], item[2])
                else:
                    _, fn, semkey, inc = item
                    fn(e).then_inc(self.sems[semkey], inc)

        @block.tensor
        def _(e):
            run(e, "pe")

        @block.scalar
        def _(e):
            run(e, "act")

        @block.vector
        def _(e):
            run(e, "dve")

        @block.gpsimd
        def _(e):
            run(e, "pool")

        @block.sync
        def _(e):
            run(e, "sp")


def build_program():
    nc = bass.Bass("TRN2", target_bir_lowering=False)

    def din(name, shape):
        return nc.dram_tensor(name, list(shape), F32, kind="ExternalInput").ap()

    def dout(name, shape):
        return nc.dram_tensor(name, list(shape), F32, kind="ExternalOutput").ap()

    x = din("x", [SEQ, D])
    meta = din("meta", [NMETA, D])
    xs = din("xs", [NS, D])
    S_in = din("S_in", [NH, HD, NS, HD])
    pool_in = din("pool_in", [NS * 15, D])
    normw_col = din("normw_col", [128, 8])
    w_in = din("w_in", [D, 6 * D])
    w_pool = din("w_pool", [4, 256, 256])
    pool_scale = din("pool_scale", [1, D])
    retw_col = din("retw_col", [128, 8])
    w_out = din("w_out", [2 * D, D])
    finw = din("finw", [1, D])
    c_ident = din("c_ident", [128, 128])
    c_cs = din("c_cs", [NT + 2, 128, 192])
    c_dec = din("c_dec", [128, 48])
    c_mask = din("c_mask", [128, 256])
    c_band = din("c_band", [128, 3 * 4 * 128])
    c_csel = din("c_csel", [128, 2 * 4 * 16])
    c_cu = din("c_cu", [128, 4 * 16])
    c_oh = din("c_oh", [128, 256])

    y = dout("y", [SEQ, D])
    ys = dout("ys", [NS, D])
    Sp = dout("Sp", [NH, HD, HD])
    Ss = dout("Ss", [NH, HD, NS, HD])
    pbp = dout("pbp", [15, D])
    pbs = dout("pbs", [NS, 15, D])

    P = Prog(nc)
    w_in_bf = P.sb("w_in_bf", [128, 8, 6 * D], BF16)
    w_out_bf = P.sb("w_out_bf", [128, 16, D], BF16)
    w_pool_bf = P.sb("w_pool_bf", [128, 2048], BF16)
    ident = P.sb("ident", [128, 128], BF16)
    normw = P.sb("normw", [128, 8], F32)
    retwc = P.sb("retwc", [128, 8], F32)
    finw_b = P.sb("finw_b", [128, D], F32)
    dec = P.sb("dec", [128, 48], F32)
    mask = P.sb("mask", [128, 256], F32)
    band = P.sb("band", [128, 3 * 4 * 128], BF16)
    csel = P.sb("csel", [128, 128], BF16)
    cu = P.sb("cu", [128, 64], BF16)
    oh = P.sb("oh", [128, 256], BF16)
    nh = P.sb("nh", [128, 8], F32)
    cs = [P.sb("cs%d" % i, [128, 192], F32) for i in range(2)]
    xall = P.sb("xall", [128, 3 * D], F32)
    xb = [xall[:, i * D:(i + 1) * D] for i in range(3)]
    Fb = P.sb("Fb", [128, 2 * D], F32)
    F = [Fb[:, i * D:(i + 1) * D] for i in range(2)]
    hn = P.sb("hn", [128, D], BF16)
    W0 = P.sb("W0", [128, D], BF16)
    hnTall = P.sb("hnTall", [128, 2 * D], BF16)
    hnTs = [hnTall[:, i * D:(i + 1) * D] for i in range(2)]
    uall = P.sb("uall", [128, 3 * D], BF16)
    u_bf = [uall[:, i * D:(i + 1) * D] for i in range(3)]
    qd = P.sb("qd", [128, D], BF16)
    kd = P.sb("kd", [128, D], BF16)
    qkT = P.sb("qkT", [128, 2 * D], BF16)
    qT = qkT[:, 0:D]
    kT = qkT[:, D:2 * D]
    v_bf = P.sb("v_bf", [128, D], BF16)
    sgpT = P.sb("sgpT", [128, D], BF16)
    G = P.sb("G", [128, D], BF16)
    ret_y = P.sb("ret_y", [128, D], BF16)
    yT = P.sb("yT", [128, 2 * D], BF16)
    S_f32 = P.sb("S_f32", [128, D], F32)
    S_bf = P.sb("S_bf", [128, D], BF16)
    ss_x = P.sb("ss_x", [128, 1], F32)
    rs_x = P.sb("rs_x", [128, 1], F32)
    ss_r = P.sb("ss_r", [128, 8], F32)
    rs_r = P.sb("rs_r", [128, 8], F32)
    ss_f = P.sb("ss_f", [128, 1], F32)
    rs_f = P.sb("rs_f", [128, 1], F32)
    pp = [P.ps("pp%d" % i, [128, D], F32) for i in range(4)]
    ppb = [t[:].bitcast(BF16) for t in pp]
    pj = [0]

    forced = []

    def next_pair():
        if forced:
            i = forced.pop(0)
        else:
            i = pj[0] % 4
            pj[0] += 1
        return pp[i], ppb[i], "pp%d" % i

    uniq = [0]

    def dma(eng, out, in_, reads, writes, sem=None):
        if sem is None:
            uniq[0] += 1
            sem = "d_u%d" % uniq[0]
        P.op(eng, lambda e: e.dma_start(out=out, in_=in_), reads=reads, writes=writes, sem=sem)

    dma("pool", ident[:], c_ident[:, :], [], ["ident"])
    dma("sp", normw[:], normw_col[:, :], [], ["normw"])
    dma("sp", retwc[:], retw_col[:, :], [], ["retwc"])
    dma("sp", dec[:], c_dec[:, :], [], ["dec"])
    dma("sp", mask[:], c_mask[:, :], [], ["mask"])
    P.op("dve", lambda e: e.memset(nh[:], -0.5), writes=["nh"])
    w_in_v = w_in.rearrange("(kc p) n -> p kc n", p=128)
    for blk in (0, 1, 6, 7, 8, 9, 4, 5, 10, 11, 2, 3):
        dma("pool", w_in_bf[:, :, blk * 512:(blk + 1) * 512], w_in_v[:, :, blk * 512:(blk + 1) * 512],
            [], ["win%d" % blk])
        if blk == 1:
            dma("pool", band[:], c_band[:, :], [], ["band"])
            dma("pool", csel[:], c_csel[:, :], [], ["csel"])
            dma("pool", cu[:], c_cu[:, :], [], ["cu"])
            dma("pool", oh[:], c_oh[:, :], [], ["oh"])
    dma("sp", finw_b[:], finw[0:1, :].broadcast_to([128, D]), [], ["finw_b"])
    dma("sp", Fb[:, 0:2048].rearrange("p (g cc d) -> p g cc d", g=4, cc=2),
        w_pool.rearrange("g (cc p) d -> p g cc d", p=128), [], ["F0", "F1"])
    dma("sp", xb[2], pool_scale[0:1, :].broadcast_to([128, D]), [], ["xb2"])
    P.op("dve", lambda e: e.tensor_tensor(
        out=w_pool_bf[:].rearrange("p (g cc d) -> p g cc d", g=4, cc=2),
        in0=Fb[:, 0:2048].rearrange("p (g cc d) -> p g cc d", g=4, cc=2),
        in1=xb[2].rearrange("p (g d) -> p g d", g=4).unsqueeze(2).broadcast_to([128, 4, 2, 256]),
        op=ALU.mult), reads=["F0", "F1", "xb2"], writes=["w_pool_bf"])
    w_out_v = w_out.rearrange("(kc p) n -> p kc n", p=128)

    def load_w_out():
        for blk in range(4):
            dma("pool", w_out_bf[:, blk * 4:(blk + 1) * 4, :], w_out_v[:, blk * 4:(blk + 1) * 4, :],
                [], ["wout%d" % blk])

    def rstd(ss, rs, n, T, inv_n, key_ss, key_rs):
        P.op("dve", lambda e: e.tensor_scalar(out=rs[:T, 0:n], in0=ss[:T, 0:n], scalar1=inv_n, scalar2=EPS,
                                              op0=ALU.mult, op1=ALU.add), reads=[key_ss], writes=[key_rs])
        P.op("pool", lambda e: e.tensor_tensor(out=rs[:T, 0:n], in0=rs[:T, 0:n], in1=nh[:T, 0:n], op=ALU.pow),
             reads=[key_rs, "nh"], writes=[key_rs])

    def act_evac(T, pt, pk, dst, dkey, func=AF.Copy, ncol=D):
        P.op("act", lambda e: e.activation(out=dst[:T, 0:ncol], in_=pt[:T, 0:ncol], func=func),
             reads=[pk], writes=[dkey])

    def dve_copy(rows, ncol, src, skeys, dst, dkeys):
        P.op("dve", lambda e: e.tensor_copy(out=dst[:rows, 0:ncol], in_=src[:rows, 0:ncol]),
             reads=list(skeys), writes=list(dkeys))

    def load_x(T, xsrc, cs_idx, xi, ci):
        dma("sp", xb[xi][:T, :], xsrc, [], ["xb%d" % xi], "d_x%d" % xi)
        dma("sp", cs[ci][:T, :], c_cs[cs_idx, 0:T, :], [], ["cs%d" % ci], "d_cs%d" % ci)

    def stage_A1(T, xi):
        xk = "xb%d" % xi
        P.op("act", lambda e: e.activation(out=hn[:T, :], in_=xb[xi][:T, :], func=AF.Square, accum_out=ss_x[:T, :]),
             reads=[xk], writes=["hn", "ss_x"])
        rstd(ss_x, rs_x, 1, T, 1.0 / D, "ss_x", "rs_x")
        P.op("dve", lambda e: e.tensor_scalar(out=hn[:T, :], in0=xb[xi][:T, :], scalar1=rs_x[:T, 0:1], scalar2=None,
                                              op0=ALU.mult), reads=[xk, "rs_x"], writes=["hn"])

    def stage_A2(T, hi):
        hnT = hnTs[hi]
        pt, ptb, pk = next_pair()

        def tr(e):
            r = None
            for kc in range(8):
                r = e.transpose(out=ptb[:, kc * T:(kc + 1) * T], in_=hn[:T, kc * 128:(kc + 1) * 128],
                                identity=ident[:T, :T])
            return r
        P.op("pe", tr, reads=["hn", "ident"], writes=[pk])
        yield

        def ev(e):
            r = None
            for kc in range(8):
                r = e.tensor_scalar(out=hnT[:, kc * T:(kc + 1) * T], in0=ptb[:, kc * T:(kc + 1) * T],
                                    scalar1=normw[:, kc:kc + 1], scalar2=None, op0=ALU.mult)
            return r
        P.op("dve", ev, reads=[pk, "normw"], writes=["hnT%d" % hi])
        yield

    def proj_A(T, col0, hi):
        hnT = hnTs[hi]
        pt, ptb, pk = next_pair()

        def mm(e):
            r = None
            for nt in range(2):
                for kc in range(8):
                    c0 = col0 + nt * 512
                    r = e.matmul(pt[:T, nt * 512:(nt + 1) * 512], lhsT=hnT[:, kc * T:(kc + 1) * T],
                                 rhs=w_in_bf[:, kc, c0:c0 + 512], start=(kc == 0), stop=(kc == 7))
            return r
        b0 = col0 // 512
        P.op("pe", mm, reads=["hnT%d" % hi, "win%d" % b0, "win%d" % (b0 + 1)], writes=[pk])
        return pt, pk

    def rotary(T, pt, pk, ci, dslot, dst, dkey, fa=0):
        Fa, fak = F[fa], "F%d" % fa

        def evac(e):
            r = None
            for h in range(NH):
                r = e.activation(out=Fa[:T, h * 128:(h + 1) * 128], in_=pt[:T, h * 128:(h + 1) * 128],
                                 func=AF.Copy, scale=dec[:T, dslot * 8 + h:dslot * 8 + h + 1])
            return r
        P.op("act", evac, reads=[pk, "dec"], writes=[fak])
        yield
        f0 = Fa[:T, :].rearrange("p (h two j) -> p h two j", h=8, two=2)
        f1 = F[1][:T, :].rearrange("p (h two j) -> p h two j", h=8, two=2)
        cosb = cs[ci][:T, 0:64].unsqueeze(1).unsqueeze(1).broadcast_to([T, 8, 2, 64])
        sinb = cs[ci][:T, 64:128].unsqueeze(1).broadcast_to([T, 8, 64])
        nsinb = cs[ci][:T, 128:192].unsqueeze(1).broadcast_to([T, 8, 64])

        def rot(e):
            e.tensor_tensor(out=f1[:, :, 0, :], in0=f0[:, :, 1, :], in1=nsinb, op=ALU.mult)
            return e.tensor_tensor(out=f1[:, :, 1, :], in0=f0[:, :, 0, :], in1=sinb, op=ALU.mult)
        P.op("dve", rot, reads=[fak, "cs%d" % ci], writes=["F1"])
        P.op("dve", lambda e: e.tensor_tensor(out=f0, in0=f0, in1=cosb, op=ALU.mult),
             reads=[fak, "F1", "cs%d" % ci], writes=[fak])
        yield
        P.op("dve", lambda e: e.tensor_tensor(out=dst[:T, :], in0=Fa[:T, :], in1=F[1][:T, :], op=ALU.add),
             reads=[fak, "F1"], writes=[dkey])
        yield

    def transposes(T, src, skey, dst, dkey, col0=0, scale_col=None, sckey=None):
        pt, ptb, pk = next_pair()

        def tr(e):
            r = None
            for h in range(NH):
                r = e.transpose(out=ptb[:, h * T:(h + 1) * T], in_=src[:T, h * 128:(h + 1) * 128],
                                identity=ident[:T, :T])
            return r
        P.op("pe", tr, reads=(list(skey) if isinstance(skey, (list, tuple)) else [skey]) + ["ident"], writes=[pk])
        if scale_col is None:
            P.op("act", lambda e: e.activation(out=dst[:, col0:col0 + 8 * T], in_=ptb[:, 0:8 * T], func=AF.Copy),
                 reads=[pk], writes=[dkey])
        else:
            def ev(e):
                r = None
                for h in range(NH):
                    r = e.tensor_scalar(out=dst[:, col0 + h * 128:col0 + h * 128 + T], in0=ptb[:, h * T:(h + 1) * T],
                                        scalar1=scale_col[:, h:h + 1], scalar2=None, op0=ALU.mult)
                return r
            P.op("dve", ev, reads=[pk, sckey], writes=[dkey])

    def out_proj_and_store(T, xi, ydst):
        xk = "xb%d" % xi
        pt, ptb, pk = next_pair()

        def mm(e):
            r = None
            for nt in range(2):
                for kc in range(16):
                    r = e.matmul(pt[:T, nt * 512:(nt + 1) * 512], lhsT=yT[:, kc * 128:kc * 128 + T],
                                 rhs=w_out_bf[:, kc, nt * 512:(nt + 1) * 512], start=(kc == 0), stop=(kc == 15))
            return r
        P.op("pe", mm, reads=["yT", "wout0", "wout1", "wout2", "wout3"], writes=[pk])
        out_post(T, xi, pt, pk, ydst)

    def out_post(T, xi, pt, pk, ydst):
        xk = "xb%d" % xi
        P.op("dve", lambda e: e.tensor_tensor(out=xb[xi][:T, :], in0=xb[xi][:T, :], in1=pt[:T, :], op=ALU.add),
             reads=[xk, pk], writes=[xk])
        P.op("act", lambda e: e.activation(out=ret_y[:T, :], in_=xb[xi][:T, :], func=AF.Square, accum_out=ss_f[:T, :]),
             reads=[xk], writes=["ret_yA", "ret_yB", "ret_yC", "ret_yD", "ss_f"])
        rstd(ss_f, rs_f, 1, T, 1.0 / D, "ss_f", "rs_f")
        P.op("dve", lambda e: e.scalar_tensor_tensor(out=xb[xi][:T, :], in0=xb[xi][:T, :], scalar=rs_f[:T, 0:1],
                                                     in1=finw_b[:T, :], op0=ALU.mult, op1=ALU.mult),
             reads=[xk, "rs_f", "finw_b"], writes=[xk])
        dma("sp", ydst, xb[xi][:T, :], [xk], [], "d_y%d" % xi)

    def ret_norm_and_gate(T, pr, prk):
        halves = [("A", range(0, 2)), ("B", range(2, 4)), ("C", range(4, 6)), ("D", range(6, 8))]
        for tag, hr in halves:
            def sq(e, hr=hr):
                r = None
                for h in hr:
                    r = e.activation(out=ret_y[:T, h * 128:(h + 1) * 128], in_=pr[:T, h * 128:(h + 1) * 128],
                                     func=AF.Square, accum_out=ss_r[:T, h:h + 1])
                return r
            P.op("act", sq, reads=[prk], writes=["ret_y" + tag, "ss_r" + tag])
        yield
        for tag, hr in halves:
            c0 = hr[0]
            P.op("dve", lambda e, c0=c0: e.tensor_scalar(out=rs_r[:T, c0:c0 + 2], in0=ss_r[:T, c0:c0 + 2],
                                                        scalar1=1.0 / HD, scalar2=EPS, op0=ALU.mult, op1=ALU.add),
                 reads=["ss_r" + tag], writes=["rs_r" + tag])
            P.op("pool", lambda e, c0=c0: e.tensor_tensor(out=rs_r[:T, c0:c0 + 2], in0=rs_r[:T, c0:c0 + 2],
                                                         in1=nh[:T, 0:2], op=ALU.pow),
                 reads=["rs_r" + tag, "nh"], writes=["rs_r" + tag])
        yield
        for tag, hr in halves:
            def gate(e, hr=hr):
                r = None
                for h in hr:
                    r = e.scalar_tensor_tensor(out=ret_y[:T, h * 128:(h + 1) * 128], in0=pr[:T, h * 128:(h + 1) * 128],
                                               scalar=rs_r[:T, h:h + 1], in1=G[:T, h * 128:(h + 1) * 128],
                                               op0=ALU.mult, op1=ALU.mult)
                return r
            P.op("dve", gate, reads=[prk, "rs_r" + tag, "G"], writes=["ret_y" + tag])
        yield

    def ret_yT(T):
        transposes(T, ret_y, ["ret_yA", "ret_yB", "ret_yC", "ret_yD"], yT, "yT", col0=8 * 128, scale_col=retwc, sckey="retwc")

    def proj_gr(T, hi):
        pt, pk = proj_A(T, 5 * D, hi)
        act_evac(T, pt, pk, G, "G", func=AF.Silu)

    def proj_gp(T, hi):
        hnT = hnTs[hi]
        pt2, ptb2, pk2 = next_pair()

        def mmB(e):
            r = None
            for fc in range(8):
                for kc in range(8):
                    r = e.matmul(pt2[:, fc * T:(fc + 1) * T], lhsT=w_in_bf[:, kc, D + fc * 128:D + (fc + 1) * 128],
                                 rhs=hnT[:, kc * T:(kc + 1) * T], start=(kc == 0), stop=(kc == 7))
            return r
        P.op("pe", mmB, reads=["hnT%d" % hi, "win2", "win3"], writes=[pk2])
        P.op("act", lambda e: e.activation(out=sgpT[:, 0:8 * T], in_=pt2[:, 0:8 * T], func=AF.Silu),
             reads=[pk2], writes=["sgpT"])

    def pooled_evac(T, ppool, pkpool):
        P.op("dve", lambda e: e.tensor_copy(out=W0[:, 0:8 * T], in_=ppool[:, 0:8 * T]), reads=[pkpool], writes=["W0"])

    def mix_and_gate(T, src=None, skey="W0"):
        src = W0 if src is None else src
        pt, ptb, pk = next_pair()

        def mm(e):
            r = None
            for i2 in range(8):
                g, dd = i2 // 2, i2 % 2
                for cc in range(2):
                    o = (g * 2 + cc) * 256 + dd * 128
                    r = e.matmul(pt[:, i2 * T:(i2 + 1) * T], lhsT=w_pool_bf[:, o:o + 128],
                                 rhs=src[:, (2 * g + cc) * T:(2 * g + cc + 1) * T], start=(cc == 0), stop=(cc == 1))
            return r
        P.op("pe", mm, reads=[skey, "w_pool_bf"], writes=[pk])
        P.op("dve", lambda e: e.tensor_tensor(
            out=yT[:, 0:8 * 128].rearrange("p (h t) -> p h t", h=8)[:, :, 0:T],
            in0=pt[:, 0:8 * T].rearrange("p (h t) -> p h t", h=8),
            in1=sgpT[:, 0:8 * T].rearrange("p (h t) -> p h t", h=8), op=ALU.mult),
            reads=[pk, "sgpT"], writes=["yT"])

    def mm_scores(T):
        pt, ptb, pk = next_pair()

        def mm(e):
            r = None
            for h in range(NH):
                r = e.matmul(pt[:T, h * T:(h + 1) * T], lhsT=kT[:, h * T:(h + 1) * T], rhs=qT[:, h * T:(h + 1) * T],
                             start=True, stop=True)
            return r
        P.op("pe", mm, reads=["kT", "qT"], writes=[pk])
        return pt, pk

    def ev_scores(T, pt, pk, scal, mcol0):
        def ev(e):
            r = None
            for h in range(NH):
                r = e.scalar_tensor_tensor(out=W0[:T, h * T:(h + 1) * T], in0=pt[:T, h * T:(h + 1) * T],
                                           scalar=float(scal[h]), in1=mask[:T, mcol0:mcol0 + T],
                                           op0=ALU.mult, op1=ALU.mult)
            return r
        P.op("dve", ev, reads=[pk, "mask"], writes=["W0"])

    def mm_dS(T):
        pt, ptb, pk = next_pair()

        def mm(e):
            r = None
            for h in range(NH):
                r = e.matmul(pt[:, h * 128:(h + 1) * 128], lhsT=kd[:T, h * 128:(h + 1) * 128],
                             rhs=v_bf[:T, h * 128:(h + 1) * 128], start=True, stop=True)
            return r
        P.op("pe", mm, reads=["kd", "v_bf"], writes=[pk])
        return pt, pk

    bandv = band[:].rearrange("p (s g t) -> p s g t", s=3, g=4)

    def tile_T(idx):
        return 128 if 1 <= idx <= NT else 16

    def tile_load(idx):
        T = tile_T(idx)
        if idx == 0:
            src = meta[:, :]
        elif idx <= NT:
            src = x[(idx - 1) * 128:idx * 128, :]
        else:
            src = xs[:, :]
        load_x(T, src, idx, idx % 3, idx % 2)

    def P1_gens(idx):
        T = tile_T(idx)
        ci, ui, hi = idx % 2, idx % 3, idx % 2
        ukey = "u_bf%d" % ui

        def job_u():
            pt, pk = proj_A(T, 0, hi)
            yield
            if idx >= NT:
                act_evac(T, pt, pk, F[1], "F1")
                dve_copy(T, D, F[1], ["F1"], u_bf[ui], [ukey])
                if idx == NT:
                    dma("sp", pbp[:, :], F[1][113:128, :], ["F1"], [])
                else:
                    dma("sp", pbs[:, 14, :], F[1][:T, :], ["F1"], [])
                    dma("sp", pbs[:, 0:14, :], pool_in.rearrange("(s j) d -> s j d", j=15)[:, 1:15, :], [], [])
            else:
                act_evac(T, pt, pk, u_bf[ui], ukey)
            yield

        def job_q():
            pt, pk = proj_A(T, 2 * D, hi)
            yield
            yield from rotary(T, pt, pk, ci, 0 if idx <= NT else 4, qd, "qd")

        def job_k():
            pt, pk = proj_A(T, 3 * D, hi)
            yield
            yield from rotary(T, pt, pk, ci, 3 if idx == 0 else (1 if idx <= NT else 5), kd, "kd")

        def job_v():
            pt, pk = proj_A(T, 4 * D, hi)
            yield
            act_evac(T, pt, pk, v_bf, "v_bf")
            yield

        def job_gr():
            proj_gr(T, hi)
            yield
            yield

        def job_gp():
            proj_gp(T, hi)
            yield
            yield
        d = {"u": job_u(), "k": job_k(), "v": job_v()}
        if idx != 0:
            d.update({"q": job_q(), "gr": job_gr(), "gp": job_gp()})
        return d

    def P2_gens(c):
        T = 128
        xi = c % 3
        ucur, ukey = u_bf[c % 3], "u_bf%d" % (c % 3)
        uprev, upkey = u_bf[(c - 1) % 3], "u_bf%d" % ((c - 1) % 3)

        def j_tr():
            transposes(T, qd, "qd", qT, "qT")
            transposes(T, kd, "kd", kT, "kT")
            yield

        def j_ds_scores():
            pd, pdk = mm_dS(T)
            yield
            pt, pk = mm_scores(T)
            ev_scores(T, pt, pk, [g ** (-128.0) for g in GAM], 0)

            def upd(e):
                r = None
                for h in range(NH):
                    r = e.scalar_tensor_tensor(out=S_f32[:, h * 128:(h + 1) * 128], in0=S_f32[:, h * 128:(h + 1) * 128],
                                               scalar=float(GAM[h] ** 128.0), in1=pd[:, h * 128:(h + 1) * 128],
                                               op0=ALU.mult, op1=ALU.add)
                return r
            P.op("dve", upd, reads=[pdk, "S_f32"], writes=["S_f32"])
            yield

        def j_ret():
            pr, prb, prk = next_pair()

            def mm_ret(e):
                r = None
                for h in range(NH):
                    e.matmul(pr[:T, h * 128:(h + 1) * 128], lhsT=W0[:T, h * T:(h + 1) * T],
                             rhs=v_bf[:T, h * 128:(h + 1) * 128], start=True, stop=False)
                    r = e.matmul(pr[:T, h * 128:(h + 1) * 128], lhsT=qT[:, h * T:(h + 1) * T],
                                 rhs=S_bf[:, h * 128:(h + 1) * 128], start=False, stop=True)
                return r
            P.op("pe", mm_ret, reads=["W0", "v_bf", "qT", "S_bf"], writes=[prk])
            yield
            g = ret_norm_and_gate(T, pr, prk)
            next(g)
            yield
            next(g)
            yield
            next(g)
            yield
            dve_copy(128, D, S_f32, ["S_f32"], S_bf, ["S_bf"])
            if c == NT:
                dma("sp", Sp.rearrange("h d v -> d h v"), S_f32[:].rearrange("p (h v) -> p h v", h=8), ["S_f32"], [])
            yield

        def j_pool():
            pq, pqb, pqk = next_pair()
            Kp, slot = (NMETA, 2) if c == 1 else (128, 1)

            def mm_pool(e):
                r = None
                for j in range(8):
                    g = j // 2
                    e.matmul(pq[:, j * T:(j + 1) * T], lhsT=uprev[:Kp, j * 128:(j + 1) * 128],
                             rhs=bandv[:Kp, slot, g, :], start=True, stop=False)
                    r = e.matmul(pq[:, j * T:(j + 1) * T], lhsT=ucur[:T, j * 128:(j + 1) * 128],
                                 rhs=bandv[:T, 0, g, :], start=False, stop=True)
                return r
            P.op("pe", mm_pool, reads=[ukey, upkey, "band"], writes=[pqk])
            pooled_evac(T, pq, pqk)
            yield

        def j_mix():
            mix_and_gate(T)
            yield
            ret_yT(T)
            yield

        def j_out():
            xk = "xb%d" % xi
            pt, ptb, pk = next_pair()

            def mm(e):
                r = None
                for nt in range(2):
                    for kc in range(16):
                        r = e.matmul(pt[:T, nt * 512:(nt + 1) * 512], lhsT=yT[:, kc * 128:kc * 128 + T],
                                     rhs=w_out_bf[:, kc, nt * 512:(nt + 1) * 512], start=(kc == 0), stop=(kc == 15))
                return r
            P.op("pe", mm, reads=["yT", "wout0", "wout1", "wout2", "wout3"], writes=[pk])
            yield
            out_post(T, xi, pt, pk, y[(c - 1) * 128:c * 128, :])
            yield
        return {"tr": j_tr(), "ds": j_ds_scores(), "ret": j_ret(), "pool": j_pool(), "mix": j_mix(), "out": j_out()}

    def P2_meta():
        pt, pk = mm_dS(NMETA)
        dve_copy(128, D, pt, [pk], S_f32, ["S_f32"])
        dve_copy(128, D, S_f32, ["S_f32"], S_bf, ["S_bf"])

    SinB = [F[0], F[1], S_f32[:, :]]
    sinK = ["F0", "F1", "S_f32"]

    def sample_load(j):
        h, half, b3 = j // 2, j % 2, j % 3
        dma("sp", SinB[b3].rearrange("p (s v) -> p s v", s=8),
            S_in[h, :, 8 * half:8 * half + 8, :], [], [sinK[b3]], "d_sl%d" % b3)

    def sample_pool_state_load():
        spb = S_f32[:].bitcast(BF16)
        pin = pool_in.rearrange("(a r) d -> r a d", a=2)
        dma("pool", spb[:120, 0:2048].rearrange("p (a d) -> p a d", a=2), pin[:, :, :], [], ["S_f32"])

    def P2_sample():
        T = NS
        idx = NT + 1
        xi = idx % 3
        transposes(T, qd, "qd", qT, "qT")
        transposes(T, kd, "kd", kT, "kT")
        pt, pk = mm_scores(T)
        ev_scores(T, pt, pk, [1.0 / g for g in GAM], 128)
        Qm = uall[:, 0:2048]
        Sin, sink = SinB, sinK
        Sout, soutk = [xb[0], xb[1]], ["xb0", "xb1"]
        Sbf, sbk = [hnTs[0], hnTs[1]], ["hnT0", "hnT1"]
        Rh, rhk = [qT, kT], ["qT", "kT"]
        P.op("dve", lambda e: e.tensor_tensor(
            out=Qm.rearrange("p (h s t) -> p h s t", h=8, s=16),
            in0=qT[:, 0:128].rearrange("p (h t) -> p h t", h=8).unsqueeze(2).broadcast_to([128, 8, 16, 16]),
            in1=oh[:, :].rearrange("p (s t) -> p s t", s=16).unsqueeze(1).broadcast_to([128, 8, 16, 16]),
            op=ALU.mult), reads=["qT", "oh"], writes=["u_bf0", "u_bf1"])
        spb = S_f32[:].bitcast(BF16)
        cselv = csel[:].rearrange("p (a g s) -> p a g s", a=2, g=4)
        cuv = cu[:].rearrange("p (g s) -> p g s", g=4)
        us = u_bf[idx % 3]
        ridx = pj[0] % 4
        pr, prb, prk = next_pair()
        pbr = (ridx + 3) % 4

        def pool_branch():
            forced.append(pbr)
            pq, pqb, pqk = next_pair()

            def mm_pool_s(e):
                r = None
                for j in range(8):
                    g = j // 2
                    e.matmul(pq[:, j * T:(j + 1) * T], lhsT=spb[:120, j * 128:(j + 1) * 128],
                             rhs=cselv[:120, 0, g, :], start=True, stop=False)
                    e.matmul(pq[:, j * T:(j + 1) * T], lhsT=spb[:120, 1024 + j * 128:1024 + (j + 1) * 128],
                             rhs=cselv[:120, 1, g, :], start=False, stop=False)
                    r = e.matmul(pq[:, j * T:(j + 1) * T], lhsT=us[:T, j * 128:(j + 1) * 128], rhs=cuv[:T, g, :],
                                 start=False, stop=True)
                return r
            P.op("pe", mm_pool_s, reads=["S_f32", "csel", "cu", "u_bf%d" % (idx % 3)], writes=[pqk])
            P.op("dve", lambda e: e.tensor_copy(out=hn[:, 0:8 * T], in_=pq[:, 0:8 * T]), reads=[pqk], writes=["hn"])
            forced.append(pbr)
            mix_and_gate(T, src=hn, skey="hn")
        qmk2 = ["u_bf0", "u_bf1"]
        NJ = 2 * NH

        def cast(j):
            bb, b3 = j % 2, j % 3
            P.op("act", lambda e: e.activation(out=Sbf[bb], in_=Sin[b3], func=AF.Copy),
                 reads=[sink[b3]], writes=[sbk[bb]])

        def ret_s(j):
            h, half, bb = j // 2, j % 2, j % 2

            def mm_ret_s(e):
                if half == 0:
                    e.matmul(pr[:T, h * 128:(h + 1) * 128], lhsT=W0[:T, h * T:(h + 1) * T],
                             rhs=v_bf[:T, h * 128:(h + 1) * 128], start=True, stop=False)
                r = None
                for sl in range(8):
                    s_ = 8 * half + sl
                    r = e.matmul(pr[:T, h * 128:(h + 1) * 128], lhsT=Qm[:, (h * 16 + s_) * 16:(h * 16 + s_ + 1) * 16],
                                 rhs=Sbf[bb][:, sl * 128:(sl + 1) * 128], start=False, stop=(s_ == NS - 1))
                return r
            P.op("pe", mm_ret_s, reads=["W0", "v_bf", sbk[bb]] + qmk2, writes=[prk])

        def rh_build(j):
            h, half, bb = j // 2, j % 2, j % 2
            P.op("dve", lambda e: e.tensor_tensor(
                out=Rh[bb][:T, :].rearrange("p (s v) -> p s v", s=8),
                in0=v_bf[:T, h * 128:(h + 1) * 128].unsqueeze(1).broadcast_to([T, 8, 128]),
                in1=mask[:T, 128 + 8 * half:128 + 8 * half + 8].unsqueeze(2).broadcast_to([T, 8, 128]),
                op=ALU.mult), reads=["v_bf", "mask"], writes=[rhk[bb]])

        def ds_upd(j):
            h, half, bb = j // 2, j % 2, j % 2
            pi = (ridx + 1 + j % 2) % 4
            pa, pak = pp[pi], "pp%d" % pi

            def mm_ds_s(e):
                r = None
                for q2 in range(2):
                    r = e.matmul(pa[:, q2 * 512:(q2 + 1) * 512], lhsT=kd[:T, h * 128:(h + 1) * 128],
                                 rhs=Rh[bb][:T, q2 * 512:(q2 + 1) * 512], start=True, stop=True)
                return r
            P.op("pe", mm_ds_s, reads=["kd", rhk[bb]], writes=[pak])
            b3 = j % 3
            P.op("dve", lambda e: e.scalar_tensor_tensor(
                out=Sout[bb], in0=Sin[b3], scalar=float(GAM[h]), in1=pa[:, :], op0=ALU.mult, op1=ALU.add),
                reads=[pak, sink[b3]], writes=[soutk[bb]])
            dma("pool", Ss[h, :, 8 * half:8 * half + 8, :],
                Sout[bb].rearrange("p (s v) -> p s v", s=8), [soutk[bb]], [], "d_ss%d" % bb)

        pool_branch()
        sample_load(2)
        cast(0)
        ret_s(0)
        rh_build(0)
        for j in range(NJ):
            if j + 1 < NJ:
                rh_build(j + 1)
            ds_upd(j)
            if j + 3 < NJ:
                sample_load(j + 3)
            if j + 1 < NJ:
                cast(j + 1)
                ret_s(j + 1)
        pj[0] = ridx + 1
        for _ in ret_norm_and_gate(T, pr, prk):
            pass
        ret_yT(T)
        out_proj_and_store(T, xi, ys[:, :])

    def run(g):
        next(g, None)

    def full(g):
        for _ in g:
            pass

    tile_load(0)
    tile_load(1)
    stage_A1(tile_T(0), 0)
    full(stage_A2(tile_T(0), 0))
    g0 = P1_gens(0)
    for k in ("u", "k", "v"):
        full(g0[k])
    stage_A1(tile_T(1), 1)
    full(stage_A2(tile_T(1), 1))
    load_w_out()
    P2_meta()
    tile_load(2)
    g1 = P1_gens(1)
    stage_A1(tile_T(2), 2 % 3)
    for k in ("u", "q", "k", "v", "gr", "gp"):
        full(g1[k])
    full(stage_A2(tile_T(2), 2 % 2))
    for idx in range(1, NT + 1):
        nA = idx + 2 if idx + 2 <= NT + 1 else None
        if nA is not None:
            tile_load(nA)
        p1 = P1_gens(idx + 1)
        p2 = P2_gens(idx)
        a2 = stage_A2(tile_T(nA), nA % 2) if nA is not None else iter(())
        pj[0] = 0
        forced.extend([0, 1]); run(p2["tr"])
        forced.append(2); run(p2["ds"])
        forced.append(3); run(p1["u"])
        forced.append(0); run(p2["ds"])
        if nA is not None:
            stage_A1(tile_T(nA), nA % 3)
        run(p1["u"])
        forced.append(1); run(p1["q"])
        forced.append(2); run(p2["ret"])
        forced.append(0); run(p2["pool"])
        run(p2["ret"])
        run(p1["q"])
        run(p2["ret"])
        run(p2["ret"])
        run(p1["q"])
        forced.append(3); run(p1["k"])
        forced.append(0); run(p2["mix"])
        run(p1["q"])
        run(p2["ret"])
        forced.append(1); run(p1["v"])
        if nA is not None:
            forced.append(2); run(a2)
        forced.append(0); run(p2["mix"])
        run(a2)
        run(p1["v"])
        run(p1["k"]); run(p1["k"]); run(p1["k"])
        if idx == NT:
            sample_load(0)
            sample_load(1)
            sample_pool_state_load()
        forced.append(1); run(p1["gr"])
        forced.append(0); run(p2["out"])
        run(p2["out"])
        forced.append(3); run(p1["gp"])
        assert not forced
    pj[0] = 0
    P2_sample()

    P.finish()
    P.emit()
    P.stack.close()
    return nc


def _constants():
    c = {}
    c["c_ident"] = np.eye(128, dtype=np.float32)
    half = HD // 2
    inv = 10000.0 ** (-np.arange(half, dtype=np.float64) / half)
    pos = np.zeros((NT + 2, 128), np.float64)
    pos[0, :NMETA] = np.arange(NMETA)
    for t in range(1, NT + 1):
        pos[t] = NMETA + (t - 1) * 128 + np.arange(128)
    pos[NT + 1, :] = PAST
    ang = (pos.astype(np.float32)[:, :, None] * inv.astype(np.float32)[None, None, :]).astype(np.float32).astype(np.float64)
    cs = np.zeros((NT + 2, 128, 192), np.float32)
    cs[:, :, 0:64] = np.cos(ang)
    cs[:, :, 64:128] = np.sin(ang)
    cs[:, :, 128:192] = -np.sin(ang)
    c["c_cs"] = cs
    g = np.array(GAM, np.float64)
    l = np.arange(128, dtype=np.float64)[:, None]
    sc = HD ** -0.5
    dec = np.zeros((128, 6, 8), np.float64)
    dec[:, 0, :] = g[None, :] ** (l + 1.0)
    dec[:, 1, :] = g[None, :] ** (127.0 - l) * sc
    dec[:, 2, :] = g[None, :] ** (l + 1.0)
    dec[:NMETA, 3, :] = g[None, :] ** (15.0 - l[:NMETA]) * sc
    dec[:, 4, :] = g[None, :]
    dec[:, 5, :] = sc
    c["c_dec"] = dec.reshape(128, 48).astype(np.float32)
    m = np.zeros((128, 256), np.float32)
    mi = np.arange(128)
    m[:, 0:128] = (mi[None, :] >= mi[:, None]).astype(np.float32)
    m[:, 128:256] = np.eye(128, dtype=np.float32)
    c["c_mask"] = m
    band = np.zeros((128, 3, 4, 128), np.float32)
    tp = np.arange(128)[:, None]
    t = np.arange(128)[None, :]
    for gi, w in enumerate(WINS):
        d = t - tp
        band[:, 0, gi, :] = ((d >= 0) & (d <= w - 1)) / w - (d == 0)
        d2 = t + 128 - tp
        band[:, 1, gi, :] = (d2 <= w - 1) / w
        d3 = t + NMETA - tp[:NMETA]
        band[:NMETA, 2, gi, :] = (d3 <= w - 1) / w
    c["c_band"] = band.reshape(128, -1).astype(np.float32)
    csel = np.zeros((128, 2, 4, 16), np.float32)
    cu = np.zeros((128, 4, 16), np.float32)
    for gi, w in enumerate(WINS):
        for a in range(2):
            for r in range(120):
                s, j = a * 8 + r // 15, r % 15
                if j >= 15 - (w - 1):
                    csel[r, a, gi, s] = 1.0 / w
        for s in range(16):
            cu[s, gi, s] = 1.0 / w - 1.0
    c["c_csel"] = csel.reshape(128, -1)
    c["c_cu"] = cu.reshape(128, -1)
    oh = np.zeros((128, 16, 16), np.float32)
    oh[:, np.arange(16), np.arange(16)] = 1.0
    oh2 = oh.reshape(128, 256).copy()
    c["c_oh"] = oh2
    return c


_CACHE = {}


def kernel(x_prompt, x_sample, state_ret, state_pool, meta_tokens, norm_w, w_in, w_pool,
           pool_scale, ret_norm_w, w_out, final_norm_w):
    f = lambda a: np.ascontiguousarray(np.asarray(a, dtype=np.float32))
    x_prompt, x_sample, state_ret, state_pool = f(x_prompt), f(x_sample), f(state_ret), f(state_pool)
    if "nc" not in _CACHE:
        _CACHE["nc"] = build_program()
        _CACHE["consts"] = _constants()
    nc = _CACHE["nc"]
    consts = _CACHE["consts"]
    shared = {
        "meta": f(meta_tokens),
        "normw_col": f(np.asarray(norm_w, np.float32).reshape(8, 128).T),
        "w_in": f(np.asarray(w_in)[0]),
        "w_pool": f(np.asarray(w_pool)[0]),
        "pool_scale": f(np.asarray(pool_scale).reshape(1, D)),
        "retw_col": f(np.asarray(ret_norm_w, np.float32).reshape(8, 128).T),
        "w_out": f(np.asarray(w_out)[0]),
        "finw": f(np.asarray(final_norm_w).reshape(1, D)),
    }
    shared.update(consts)
    in_maps = []
    for c in range(8):
        m = dict(shared)
        m["x"] = x_prompt[c]
        m["xs"] = f(x_sample[c * NS:(c + 1) * NS, 0, :])
        m["S_in"] = f(state_ret[0, c * NS:(c + 1) * NS].transpose(1, 2, 0, 3))
        m["pool_in"] = f(state_pool[0, c * NS:(c + 1) * NS].reshape(NS * 15, D))
        in_maps.append(m)
    res = run_bass_kernel_spmd(nc, in_maps, core_ids=list(range(8)))
    R = res.results
    y_prompt = np.stack([R[c]["y"] for c in range(8)], 0).astype(np.float32)
    y_sample = np.concatenate([R[c]["ys"] for c in range(8)], 0).reshape(8 * NS, 1, D).astype(np.float32)
    ret_p = np.stack([R[c]["Sp"] for c in range(8)], 0)[None].astype(np.float32)
    ret_s = np.concatenate([np.asarray(R[c]["Ss"]).transpose(2, 0, 1, 3) for c in range(8)], 0)[None].astype(np.float32)
    pb_p = np.stack([R[c]["pbp"] for c in range(8)], 0)[None].astype(np.float32)
    pb_s = np.concatenate([R[c]["pbs"] for c in range(8)], 0)[None].astype(np.float32)
    return (y_prompt, y_sample, ret_p, ret_s, pb_p, pb_s)
```
